# Optimizing a Trainium2 kernel written in Bass

```python
import math
import jax
import jax.numpy as jnp
from jax import lax
import numpy as np

D_MODEL = 2048
BATCH = 4
SEQ = 2048
DEPTH = 4
DEC_BATCH = 128
DEC_SEQ = 1
PAST_LEN = 16384
PAGE_SIZE = 128

N_EVEN = (DEPTH + 1) // 2
N_ODD = DEPTH // 2
POOL_CH = D_MODEL // 2
POOL_WINDOWS = (2, 4, 8, 16)
POOL_GROUPS = len(POOL_WINDOWS)
POOL_GROUP_CH = POOL_CH // POOL_GROUPS
POOL_HIST = max(POOL_WINDOWS) - 1
SGU_CH = D_MODEL // 2
SGU_HEADS = 4
SGU_HEAD_CH = SGU_CH // SGU_HEADS
SGU_CHUNK = 128
EVEN_IN = POOL_CH + 2 * SGU_CH
EVEN_OUT = POOL_CH + SGU_CH
CONV_CH = D_MODEL // 2
CONV_WIDTH = 31
DN_HEADS = 8
DN_DK = 128
DN_DV = 128
DN_QK = DN_HEADS * DN_DK
DN_V = DN_HEADS * DN_DV
DN_CONV_CH = 2 * DN_QK + DN_V
DN_CONV = 4
DN_CHUNK = 64
ODD_IN = 2 * CONV_CH + DN_CONV_CH + DN_V + 2 * DN_HEADS
ODD_OUT = CONV_CH + DN_V
D_FF = -(-8 * D_MODEL // (3 * 256)) * 256
EPS = 1e-6

kernel_name = 'hybrid_pool_sgu_conformer_gdn_decode_step'


def _rmsnorm(x, g):
    xf = x.astype(jnp.float32)
    y = xf * lax.rsqrt(jnp.mean(xf * xf, -1, keepdims=True) + EPS)
    return (y * g.astype(jnp.float32)).astype(x.dtype)


def _layernorm(x, g, b):
    xf = x.astype(jnp.float32)
    mu = jnp.mean(xf, -1, keepdims=True)
    var = jnp.mean(jnp.square(xf - mu), -1, keepdims=True)
    y = (xf - mu) * lax.rsqrt(var + EPS) * g.astype(jnp.float32) + b.astype(jnp.float32)
    return y.astype(x.dtype)


def _l2norm(x):
    return x * lax.rsqrt(jnp.sum(x * x, -1, keepdims=True) + EPS)


def _causal_dwconv(xh, w):
    ch = w.shape[1]
    return lax.conv_general_dilated(xh, w[:, None, :].astype(xh.dtype), window_strides=(1,), padding='VALID',
                                    dimension_numbers=('NWC', 'WIO', 'NWC'), feature_group_count=ch)


def _pool_mixer(a, hist, pos0, w_grp, scale):
    bsz, t_len, ch = a.shape
    xh = jnp.concatenate([hist.astype(a.dtype), a], 1)
    ah = xh.astype(jnp.float32)
    cs = jnp.concatenate([jnp.zeros((bsz, 1, ch), jnp.float32), jnp.cumsum(ah, axis=1)], 1)
    pos = pos0 + jnp.arange(t_len)
    off = POOL_HIST + 1
    outs = []
    for gi, w in enumerate(POOL_WINDOWS):
        sl = slice(gi * POOL_GROUP_CH, (gi + 1) * POOL_GROUP_CH)
        s = cs[:, off:off + t_len, sl] - cs[:, off - w:off - w + t_len, sl]
        cnt = jnp.minimum(w, pos + 1).astype(jnp.float32)[None, :, None]
        outs.append(s / cnt)
    pooled = (jnp.concatenate(outs, -1) - a.astype(jnp.float32)).astype(a.dtype)
    pooled = pooled.reshape(bsz, t_len, POOL_GROUPS, POOL_GROUP_CH)
    mixed = jnp.einsum('btgc,gcd->btgd', pooled, w_grp).reshape(bsz, t_len, ch)
    return mixed * scale, xh[:, -POOL_HIST:]


def _sgu_mixer(u, v, ln_g, ln_b, ws, bs):
    bsz, t_len, _ = u.shape
    vn = _layernorm(v, ln_g, ln_b)
    n = -(-t_len // SGU_CHUNK)
    tp = n * SGU_CHUNK
    vp = jnp.pad(vn, ((0, 0), (0, tp - t_len), (0, 0))).reshape(bsz, n, SGU_CHUNK, SGU_HEADS, SGU_HEAD_CH)
    causal = jnp.tril(jnp.ones((SGU_CHUNK, SGU_CHUNK), bool))
    wm = jnp.where(causal[None], ws, jnp.zeros_like(ws))
    mixed = jnp.einsum('hij,bnjhc->bnihc', wm, vp) + bs.T[None, None, :, :, None]
    mixed = mixed.reshape(bsz, tp, SGU_CH)[:, :t_len]
    return u * mixed, vn


def _conv_module(zc, hist, dw, db, ln_g, ln_b):
    glu = zc[..., :CONV_CH] * jax.nn.sigmoid(zc[..., CONV_CH:])
    xh = jnp.concatenate([hist.astype(glu.dtype), glu], 1)
    y = _causal_dwconv(xh, dw) + db
    y = jax.nn.silu(_layernorm(y, ln_g, ln_b))
    return y, xh[:, -(CONV_WIDTH - 1):]


def _to_chunks(t, n, c):
    bsz = t.shape[0]
    t = t.reshape((bsz, n, c) + t.shape[2:])
    return t.transpose((1, 0, 3, 2) + tuple(range(4, t.ndim)))


def _gated_delta(q, k, v, g, beta, s0):
    bsz, t_len, nh, _ = q.shape
    c = min(DN_CHUNK, t_len)
    n = -(-t_len // c)
    tp = n * c

    def padt(t):
        return jnp.pad(t, [(0, 0), (0, tp - t_len)] + [(0, 0)] * (t.ndim - 2))

    q, k, v, g, beta = [_to_chunks(padt(t), n, c) for t in (q, k, v, g, beta)]
    gc = jnp.cumsum(g, axis=-1)
    lower = jnp.tril(jnp.ones((c, c), bool))
    strict = jnp.tril(jnp.ones((c, c), bool), -1)
    decay = jnp.exp(jnp.where(lower, gc[..., :, None] - gc[..., None, :], -jnp.inf))
    kb = k * beta[..., None]
    vb = v * beta[..., None]
    lmat = jnp.where(strict, jnp.einsum('nbhid,nbhjd->nbhij', kb, k) * decay, 0.0)
    amat = lmat + jnp.eye(c, dtype=jnp.float32)
    rhs = jnp.concatenate([vb, kb * jnp.exp(gc)[..., None]], -1)
    sol = lax.linalg.triangular_solve(amat, rhs, left_side=True, lower=True, unit_diagonal=True)
    u_c, w_c = sol[..., :DN_DV], sol[..., DN_DV:]
    qk = jnp.einsum('nbhid,nbhjd->nbhij', q, k) * decay
    q_dec = q * jnp.exp(gc)[..., None]
    k_dec = k * jnp.exp(gc[..., -1:] - gc)[..., None]
    g_last = jnp.exp(gc[..., -1])

    def step(s, xs):
        u_i, w_i, qk_i, qd_i, kd_i, gl_i = xs
        v_new = u_i - jnp.einsum('bhcd,bhde->bhce', w_i, s)
        o = jnp.einsum('bhcd,bhde->bhce', qd_i, s) + jnp.einsum('bhij,bhje->bhie', qk_i, v_new)
        s = s * gl_i[..., None, None] + jnp.einsum('bhcd,bhce->bhde', kd_i, v_new)
        return s, o

    s_fin, o = lax.scan(step, s0, (u_c, w_c, qk, q_dec, k_dec, g_last))
    o = o.transpose(1, 0, 3, 2, 4).reshape(bsz, tp, nh, DN_DV)[:, :t_len]
    return o, s_fin


def _gated_deltanet(qkv, zg, a, b, hist, s0, conv_w, a_log, dt_bias, norm_g):
    bsz, t_len, _ = qkv.shape
    xh = jnp.concatenate([hist.astype(qkv.dtype), qkv], 1)
    qkv_c = jax.nn.silu(_causal_dwconv(xh, conv_w)).astype(jnp.float32)
    q = qkv_c[..., :DN_QK].reshape(bsz, t_len, DN_HEADS, DN_DK)
    k = qkv_c[..., DN_QK:2 * DN_QK].reshape(bsz, t_len, DN_HEADS, DN_DK)
    v = qkv_c[..., 2 * DN_QK:].reshape(bsz, t_len, DN_HEADS, DN_DV)
    q = _l2norm(q) * (DN_DK ** -0.5)
    k = _l2norm(k)
    g = -jnp.exp(a_log.astype(jnp.float32)) * jax.nn.softplus(a.astype(jnp.float32) + dt_bias.astype(jnp.float32))
    beta = jax.nn.sigmoid(b.astype(jnp.float32))
    o, s_fin = _gated_delta(q, k, v, g, beta, s0.astype(jnp.float32))
    o = _rmsnorm(o, norm_g) * jax.nn.silu(zg.astype(jnp.float32).reshape(bsz, t_len, DN_HEADS, DN_DV))
    return o.reshape(bsz, t_len, DN_V).astype(qkv.dtype), xh[:, -(DN_CONV - 1):], s_fin.astype(s0.dtype)


def _trunk(x, pool_st, convc_st, dnconv_st, dns_st, pos0, p):
    new_pool, new_v, new_cc, new_dc, new_s = [], [], [], [], []
    for l in range(DEPTH):
        h = _rmsnorm(x, p['norm_mix'][l])
        if l % 2 == 0:
            e = l // 2
            z = h @ p['ev_w_in'][e]
            a_out, pool_hist = _pool_mixer(z[..., :POOL_CH], pool_st[e], pos0, p['pool_w'][e], p['pool_scale'][e])
            uv = jax.nn.gelu(z[..., POOL_CH:], approximate=False)
            b_out, v_rows = _sgu_mixer(uv[..., :SGU_CH], uv[..., SGU_CH:], p['sgu_ln_g'][e], p['sgu_ln_b'][e],
                                       p['sgu_ws'][e], p['sgu_b'][e])
            mix = jnp.concatenate([a_out, b_out], -1) @ p['ev_w_out'][e]
            new_pool.append(pool_hist)
            new_v.append(v_rows)
        else:
            o = l // 2
            z = h @ p['od_w_in'][o]
            c0 = 2 * CONV_CH
            c1 = c0 + DN_CONV_CH
            c2 = c1 + DN_V
            c3 = c2 + DN_HEADS
            c_out, cc = _conv_module(z[..., :c0], convc_st[o], p['cv_dw'][o], p['cv_db'][o], p['cv_ln_g'][o], p['cv_ln_b'][o])
            d_out, dc, s = _gated_deltanet(z[..., c0:c1], z[..., c1:c2], z[..., c2:c3], z[..., c3:], dnconv_st[o], dns_st[o],
                                           p['dn_conv_w'][o], p['dn_a_log'][o], p['dn_dt_bias'][o], p['dn_norm_g'][o])
            mix = jnp.concatenate([c_out, d_out], -1) @ p['od_w_out'][o]
            new_cc.append(cc)
            new_dc.append(dc)
            new_s.append(s)
        x = x + mix
        h = _rmsnorm(x, p['norm_ffn'][l])
        gu = h @ p['ffn_w_up'][l]
        x = x + (jax.nn.silu(gu[..., :D_FF]) * gu[..., D_FF:]) @ p['ffn_w_down'][l]
    y = _rmsnorm(x, p['norm_final'])
    return y, jnp.stack(new_pool), jnp.stack(new_v), jnp.stack(new_cc), jnp.stack(new_dc), jnp.stack(new_s)


def setup_inputs(seed: int = 0) -> dict:
    key = jax.random.key(seed)
    ks = iter(jax.random.split(key, 64))

    def nrm(shape, scale):
        return jax.random.normal(next(ks), shape, jnp.float32) * scale

    def gain(shape):
        return 1.0 + nrm(shape, 0.05)

    dt = jnp.exp(jax.random.uniform(next(ks), (N_ODD, DN_HEADS), jnp.float32, math.log(1e-3), math.log(1e-1)))
    return {
        'x_prompt': nrm((BATCH, SEQ, D_MODEL), 1.0),
        'x_sample': nrm((DEC_BATCH, DEC_SEQ, D_MODEL), 1.0),
        'state_pool': nrm((N_EVEN, DEC_BATCH, POOL_HIST, POOL_CH), 1.0),
        'state_conv_c': nrm((N_ODD, DEC_BATCH, CONV_WIDTH - 1, CONV_CH), 0.5),
        'state_dn_conv': nrm((N_ODD, DEC_BATCH, DN_CONV - 1, DN_CONV_CH), 1.0),
        'state_dn_S': nrm((N_ODD, DEC_BATCH, DN_HEADS, DN_DK, DN_DV), 0.5),
        'norm_mix': gain((DEPTH, D_MODEL)),
        'norm_ffn': gain((DEPTH, D_MODEL)),
        'norm_final': gain((D_MODEL,)),
        'ev_w_in': nrm((N_EVEN, D_MODEL, EVEN_IN), D_MODEL ** -0.5),
        'pool_w': nrm((N_EVEN, POOL_GROUPS, POOL_GROUP_CH, POOL_GROUP_CH), POOL_GROUP_CH ** -0.5),
        'pool_scale': gain((N_EVEN, POOL_CH)),
        'sgu_ln_g': gain((N_EVEN, SGU_CH)),
        'sgu_ln_b': nrm((N_EVEN, SGU_CH), 0.02),
        'sgu_ws': nrm((N_EVEN, SGU_HEADS, SGU_CHUNK, SGU_CHUNK), SGU_CHUNK ** -0.5),
        'sgu_b': 1.0 + nrm((N_EVEN, SGU_HEADS, SGU_CHUNK), 0.02),
        'ev_w_out': nrm((N_EVEN, EVEN_OUT, D_MODEL), EVEN_OUT ** -0.5),
        'od_w_in': nrm((N_ODD, D_MODEL, ODD_IN), D_MODEL ** -0.5),
        'cv_dw': nrm((N_ODD, CONV_WIDTH, CONV_CH), CONV_WIDTH ** -0.5),
        'cv_db': nrm((N_ODD, CONV_CH), 0.02),
        'cv_ln_g': gain((N_ODD, CONV_CH)),
        'cv_ln_b': nrm((N_ODD, CONV_CH), 0.02),
        'dn_conv_w': nrm((N_ODD, DN_CONV, DN_CONV_CH), DN_CONV ** -0.5),
        'dn_a_log': jnp.log(jax.random.uniform(next(ks), (N_ODD, DN_HEADS), jnp.float32, 1.0, 16.0)),
        'dn_dt_bias': dt + jnp.log(-jnp.expm1(-dt)),
        'dn_norm_g': gain((N_ODD, DN_DV)),
        'od_w_out': nrm((N_ODD, ODD_OUT, D_MODEL), ODD_OUT ** -0.5),
        'ffn_w_up': nrm((DEPTH, D_MODEL, 2 * D_FF), D_MODEL ** -0.5),
        'ffn_w_down': nrm((DEPTH, D_FF, D_MODEL), D_FF ** -0.5),
    }


def reference(x_prompt, x_sample, state_pool, state_conv_c, state_dn_conv, state_dn_S,
              norm_mix, norm_ffn, norm_final, ev_w_in, pool_w, pool_scale, sgu_ln_g, sgu_ln_b, sgu_ws, sgu_b,
              ev_w_out, od_w_in, cv_dw, cv_db, cv_ln_g, cv_ln_b, dn_conv_w, dn_a_log, dn_dt_bias, dn_norm_g,
              od_w_out, ffn_w_up, ffn_w_down):
    p = dict(norm_mix=norm_mix, norm_ffn=norm_ffn, norm_final=norm_final, ev_w_in=ev_w_in, pool_w=pool_w,
             pool_scale=pool_scale, sgu_ln_g=sgu_ln_g, sgu_ln_b=sgu_ln_b, sgu_ws=sgu_ws, sgu_b=sgu_b,
             ev_w_out=ev_w_out, od_w_in=od_w_in, cv_dw=cv_dw, cv_db=cv_db, cv_ln_g=cv_ln_g, cv_ln_b=cv_ln_b,
             dn_conv_w=dn_conv_w, dn_a_log=dn_a_log, dn_dt_bias=dn_dt_bias, dn_norm_g=dn_norm_g,
             od_w_out=od_w_out, ffn_w_up=ffn_w_up, ffn_w_down=ffn_w_down)
    bp = x_prompt.shape[0]
    dt = x_prompt.dtype
    pool0 = jnp.zeros((N_EVEN, bp, POOL_HIST, POOL_CH), dt)
    cc0 = jnp.zeros((N_ODD, bp, CONV_WIDTH - 1, CONV_CH), dt)
    dc0 = jnp.zeros((N_ODD, bp, DN_CONV - 1, DN_CONV_CH), dt)
    s0 = jnp.zeros((N_ODD, bp, DN_HEADS, DN_DK, DN_DV), state_dn_S.dtype)
    y_prompt, pool_p, _, cc_p, dc_p, s_p = _trunk(x_prompt, pool0, cc0, dc0, s0, 0, p)
    y_sample, pool_s, v_s, cc_s, dc_s, s_s = _trunk(x_sample, state_pool, state_conv_c, state_dn_conv, state_dn_S,
                                                    PAST_LEN, p)
    return (y_prompt, y_sample, pool_p, pool_s, v_s, cc_p, cc_s, dc_p, dc_s, s_p, s_s)
```

```python
import numpy as np
import concourse.bass as bass
import concourse.mybir as mybir
from concourse.bass_utils import run_bass_kernel_spmd
from contextlib import ExitStack

F32 = mybir.dt.float32
BF16 = mybir.dt.bfloat16
AF = mybir.ActivationFunctionType
ALU = mybir.AluOpType
AX = mybir.AxisListType

D = 2048
DEPTH = 4
T = 512
import os
NRUN = int(os.environ.get('K_NRUN', '4'))
NSMP = 16
NCOL = T + NSMP
DFF = 5632
EPS = 1e-6
RING = 4
USE_WCACHE = True
PLAYOUT = {}
DBG_LAYERS = int(os.environ.get('K_LAYERS', '4'))


class Sched:
    def __init__(self):
        self.ops = []

    def add(self, eng, meth, args, kwargs, R, W, dma=False):
        self.ops.append([eng, meth, args, kwargs, tuple(R), tuple(W), dma])

    def emit(self, nc, es):
        ops = self.ops
        n = len(ops)
        lastw = {}
        readers = {}
        deps = [None] * n
        needed = [False] * n
        for i, (eng, meth, args, kw, R, W, dma) in enumerate(ops):
            d = set()
            for k in R:
                j = lastw.get(k)
                if j is not None:
                    d.add(j)
                if k[0] == 'ps' or k[0] == 'psb':
                    for j in readers.get(k, ()):
                        if ops[j][0] != eng:
                            d.add(j)
            for k in W:
                j = lastw.get(k)
                if j is not None:
                    d.add(j)
                for j in readers.get(k, ()):
                    d.add(j)
            for k in R:
                readers.setdefault(k, []).append(i)
            for k in W:
                lastw[k] = i
                readers[k] = []
            d.discard(i)
            dd = []
            for j in d:
                if ops[j][0] == 'pe' and eng == 'pe':
                    continue
                dd.append(j)
                needed[j] = True
            deps[i] = dd
        engs = ['pe', 'act', 'dve', 'pool', 'sp']
        NDS = 12
        csem = {e: es.enter_context(nc.semaphore("c_" + e)) for e in engs}
        dsem = {e: [es.enter_context(nc.semaphore("d_%s%d" % (e, i))) for i in range(NDS)] for e in ('pool', 'sp', 'act')}
        ccount = {e: 0 for e in engs}
        dcount = {e: 0 for e in dsem}
        sig = [None] * n
        prevdma = [None] * n
        for i, op in enumerate(ops):
            eng = op[0]
            if op[6]:
                m = dcount[eng]
                dcount[eng] += 1
                s = dsem[eng][m % NDS]
                sig[i] = (s, 16 * (m // NDS + 1), 16)
                if m >= NDS:
                    prevdma[i] = (s, 16 * (m // NDS))
            elif needed[i]:
                ccount[eng] += 1
                sig[i] = (csem[eng], ccount[eng], 1)
        per = {e: [] for e in engs}
        for i, op in enumerate(ops):
            per[op[0]].append(i)
        finals = []
        for e in dsem:
            for k, s in enumerate(dsem[e]):
                cnt = (dcount[e] - k + NDS - 1) // NDS if dcount[e] > k else 0
                if cnt > 0:
                    finals.append((s, 16 * cnt))
        block = es.enter_context(nc.Block())

        def run_engine(engobj, e):
            waited = {}
            for i in per[e]:
                eng, meth, args, kw, R, W, dma = ops[i]
                wl = {}
                for j in deps[i]:
                    s, v, _ = sig[j]
                    key = id(s)
                    if key not in wl or wl[key][1] < v:
                        wl[key] = (s, v)
                if prevdma[i] is not None:
                    s, v = prevdma[i]
                    key = id(s)
                    if key not in wl or wl[key][1] < v:
                        wl[key] = (s, v)
                for key, (s, v) in wl.items():
                    if waited.get(key, 0) >= v:
                        continue
                    engobj.wait_ge(s, v)
                    waited[key] = v
                ins = getattr(engobj, meth)(*args, **kw)
                if sig[i] is not None:
                    ins.then_inc(sig[i][0], sig[i][2])
            if e == 'sp':
                for s, v in finals:
                    engobj.wait_ge(s, v)

        @block.tensor
        def _(eng):
            run_engine(eng, 'pe')

        @block.scalar
        def _(eng):
            run_engine(eng, 'act')

        @block.vector
        def _(eng):
            run_engine(eng, 'dve')

        @block.gpsimd
        def _(eng):
            run_engine(eng, 'pool')

        @block.sync
        def _(eng):
            run_engine(eng, 'sp')


class Arena:
    def __init__(self, nc, es, name, nelem, dtype):
        self.t = es.enter_context(nc.sbuf_tensor(name, [128, nelem], dtype))
        self.name = name
        self.n = nelem
        self.off = 0

    def alloc(self, d1, d2, at=None):
        if at is None:
            at = self.off
            self.off = at + d1 * d2
        assert at + d1 * d2 <= self.n, (self.name, at, d1, d2, self.n)
        return View(self, at, d1, d2)


class View:
    BLK = 64

    def __init__(self, ar, off, d1, d2):
        self.ar, self.off, self.d1, self.d2 = ar, off, d1, d2

    def a(self, i, lo=0, hi=None, p0=0, p1=128):
        hi = self.d2 if hi is None else hi
        b = self.off + i * self.d2
        return self.ar.t[p0:p1, b + lo:b + hi]

    def a3(self, i0, i1, lo=0, hi=None, p0=0, p1=128):
        hi = self.d2 if hi is None else hi
        b = self.off
        v = self.ar.t[p0:p1, b + i0 * self.d2:b + i1 * self.d2].rearrange("p (a b) -> p a b", b=self.d2)
        return v[:, :, lo:hi]

    def k(self, i0, i1=None, lo=0, hi=None):
        hi = self.d2 if hi is None else hi
        i1 = i0 + 1 if i1 is None else i1
        ks = []
        for i in range(i0, i1):
            b = self.off + i * self.d2
            for blk in range((b + lo) // self.BLK, (b + hi - 1) // self.BLK + 1):
                ks.append((self.ar.name, blk))
        return ks

    def kall(self):
        return self.k(0, self.d1)


def build_program():
    nc = bass.Bass('TRN2', target_bir_lowering=False)
    S = Sched()
    es = ExitStack()

    def din(name, shape):
        return nc.dram_tensor(name, list(shape), F32, kind="ExternalInput").ap()

    def dout(name, shape):
        return nc.dram_tensor(name, list(shape), F32, kind="ExternalOutput").ap()

    xin = din("xin", [128, 16, 2048 + NSMP])
    yout = dout("yout", [128, 16, 2048 + NSMP])
    spool = din("spool", [2, 128, 8 * NSMP * 16])
    sconv = din("sconv", [2, 128, 8 * NSMP * 31])
    sdnc = din("sdnc", [2, 128, 24 * NSMP * 4])
    sS = din("sS", [2, NSMP, 128, 8 * 128])
    o_pool_p = dout("o_pool_p", [2, 128, 8 * 15])
    o_pool_s = dout("o_pool_s", [2, 128, 8 * NSMP * 16])
    o_v_s = dout("o_v_s", [2, NSMP, 1024])
    o_conv_p = dout("o_conv_p", [2, 128, 8 * 30])
    o_conv_s = dout("o_conv_s", [2, 128, 8 * NSMP * 31])
    o_dnc_p = dout("o_dnc_p", [2, 128, 24 * 3])
    o_dnc_s = dout("o_dnc_s", [2, 128, 24 * NSMP * 4])
    o_S_p = dout("o_S_p", [2, 128, 8 * 128])
    o_S_s = dout("o_S_s", [2, NSMP, 128, 8 * 128])
    ev_w_in = din("ev_w_in", [2, D, 3072])
    ev_w_out = din("ev_w_out", [2, D, D])
    od_w_in = din("od_w_in", [2, D, 6160])
    od_w_out = din("od_w_out", [2, D, D])
    ffn_w_up = din("ffn_w_up", [4, D, 2 * DFF])
    ffn_w_down = din("ffn_w_down", [4, DFF, D])
    NPF = 16 * 9 + 2 * 8 + 2 * 8 * 31 + 2 * 8 * 3 + 2 * 24 * 4 + 2 + 2 * 8 * 3
    prm = din("prm", [128, 2048])
    lnrep = din("lnrep", [2, 2, 128, 1024])
    poolw = din("poolw", [2, 128, 4 * 2 * 256])
    wsT = din("wsT", [2, 128, 4 * 128])
    cst = din("cst", [128, 6 * 128 + 4 * 2 * 16])
    NSLAB_MAX = 420
    wcaches = [nc.dram_tensor("wcache%d" % i, [210, 128, 16 * 256], BF16, kind="Internal").ap() for i in range(2)]

    AF_N = 32200
    AB_N = 16 * NCOL + RING * 16 * 256 + 16896 + 128
    arf = Arena(nc, es, "arf", AF_N, F32)
    arb = Arena(nc, es, "arb", AB_N, BF16)
    xT = arf.alloc(16, NCOL)
    PRM = arf.alloc(1, 2048)
    CST = arf.alloc(1, 6 * 128 + 128)
    PH = arf.alloc(2, 8 * 15)
    CH = arf.alloc(2, 8 * 30)
    DH = arf.alloc(2, 24 * 3)
    SST = arf.alloc(2, 8 * 128)
    scr_f0 = arf.off
    hT = arb.alloc(16, NCOL)
    ringv = [arb.alloc(16, 256) for _ in range(RING)]
    IDB = arb.alloc(1, 128)
    scr_b0 = arb.off
    pss = [es.enter_context(nc.psum_tensor("ps%d" % i, [128, 512], F32)) for i in range(7)]
    psb = es.enter_context(nc.psum_tensor("psb", [128, 1024], BF16))

    def PE(meth, args, R, W, **kw):
        S.add('pe', meth, args, kw, R, W)

    def ACT(meth, args, R, W, **kw):
        S.add('act', meth, args, kw, R, W)

    def DVE(meth, args, R, W, **kw):
        S.add('dve', meth, args, kw, R, W)

    def DMA(eng, out, in_, R, W):
        S.add(eng, 'dma_start', (), dict(out=out, in_=in_), R, W, dma=True)

    bank_ctr = [0]

    def nbank():
        b = bank_ctr[0] % 7
        bank_ctr[0] += 1
        return b

    def pk(b):
        return [('ps', b, s_) for s_ in range(4)]

    def pks(b, s_):
        return [('ps', b, q_) for q_ in range(4)]

    ring_ctr = [0]
    cur_run = [0]
    slab_seq = [0]

    def load_slab(Wap, nkc, f0, fw):
        r = ring_ctr[0] % RING
        ring_ctr[0] += 1
        rv = ringv[r]
        sq = slab_seq[0]
        slab_seq[0] += 1
        assert sq < NSLAB_MAX
        n_el = nkc * 256
        whole = rv.ar.t[:, rv.off:rv.off + n_el]
        wk = rv.k(0, nkc)
        if cur_run[0] == 0 or not USE_WCACHE:
            src = Wap.rearrange("(kc p) f -> p kc f", p=128)[:, 0:nkc, f0:f0 + fw]
            dst = rv.a3(0, nkc, 0, fw)
            DMA('pool', dst, src, [], wk)
            if USE_WCACHE and NRUN > 1:
                if fw < 256:
                    pass
                DMA('sp', wcaches[sq // 210][sq % 210, :, 0:n_el], whole, wk, [('wc', sq)])
        else:
            DMA('sp', whole, wcaches[sq // 210][sq % 210, :, 0:n_el], [('wc', sq)], wk)
        return rv

    poff = {}
    pcur = [0]

    def pdef(name, n):
        poff[name] = pcur[0]
        pcur[0] += n

    for l in range(4):
        pdef("nmix%d" % l, 16)
        pdef("nffn%d" % l, 16)
    pdef("nfin", 16)
    for e in range(2):
        pdef("pscale%d" % e, 8)
        pdef("bs0_%d" % e, 4)
        pdef("w00_%d" % e, 4)
        pdef("bsrep%d" % e, 4 * 128)
    for o in range(2):
        pdef("cw%d" % o, 8 * 31)
        pdef("cdb%d" % o, 8)
        pdef("clg%d" % o, 8)
        pdef("clb%d" % o, 8)
        pdef("dw%d" % o, 24 * 4)
        pdef("ng%d" % o, 1)
        pdef("alog%d" % o, 8)
        pdef("dtb%d" % o, 8)
    pdef("eps", 1)
    assert pcur[0] <= 2048, pcur[0]
    PLAYOUT.update(poff)

    def P(name, i=0, n=1):
        b = poff[name] + i
        return PRM.a(0, b, b + n)

    PK = PRM.kall()
    C_ID, C_UT, C_SL, C_ONE = 0, 128, 256, 384
    C_INV = 768

    def CS(off, n=128, p1=128):
        return CST.a(0, off, off + n, 0, p1)

    CK = CST.kall()

    DMA('sp', PRM.a(0), prm[:, :], [], PK)
    DMA('sp', CST.a(0, 0, 6 * 128 + 128), cst[:, :], [], CK)
    DMA('pool', IDB.a(0), cst[:, 0:128], [], IDB.kall())

    def colchunks(run):
        return [(0, T)] + ([(T, NSMP)] if run == 0 else [])

    def rmsnorm(run, gname, out_bf16=True, outv=None):
        fo = scr_f0
        SQ = [View(arf, fo, 1, 512), View(arf, fo + 512, 1, 512)]
        RS = View(arf, fo + 1024, 1, 512)
        for (c0, cw) in colchunks(run):
            b = nbank()
            for kc in range(16):
                sq = SQ[kc % 2]
                ACT('activation', (sq.a(0, 0, cw), xT.a(kc, c0, c0 + cw), AF.Square), xT.k(kc, None, c0, c0 + cw), sq.k(0, None, 0, cw))
                PE('matmul', (pss[b][:, 0:cw], CS(C_ONE), sq.a(0, 0, cw)), sq.k(0, None, 0, cw) + CK, pk(b), start=(kc == 0), stop=(kc == 15))
            ACT('activation', (RS.a(0, 0, cw), pss[b][:, 0:cw], AF.Sqrt), pk(b) + PK, RS.k(0, None, 0, cw), scale=1.0 / D, bias=P("eps"))
            DVE('reciprocal', (RS.a(0, 0, cw), RS.a(0, 0, cw)), RS.k(0, None, 0, cw), RS.k(0, None, 0, cw))
            for kc in range(16):
                if out_bf16:
                    o_ap, o_k = hT.a(kc, c0, c0 + cw), hT.k(kc, None, c0, c0 + cw)
                else:
                    o_ap, o_k = outv.a(kc, c0, c0 + cw), outv.k(kc, None, c0, c0 + cw)
                DVE('scalar_tensor_tensor', (o_ap, xT.a(kc, c0, c0 + cw), P(gname, kc), RS.a(0, 0, cw), ALU.mult, ALU.mult),
                    xT.k(kc, None, c0, c0 + cw) + RS.k(0, None, 0, cw) + PK, o_k)

    def stream_fm(run, Wap, nkc, slabs, act, epi, after_slab=None):
        for (f0, fw) in slabs:
            if after_slab is not None:
                after_slab()
            rv = load_slab(Wap, nkc, f0, fw)
            for j in range((fw + 127) // 128):
                M = min(128, fw - j * 128)
                for (c0, cw) in colchunks(run):
                    b = nbank()
                    for kc in range(nkc):
                        PE('matmul', (pss[b][0:M, 0:cw], rv.a(kc, j * 128, j * 128 + M), act.a(kc, c0, c0 + cw)),
                           rv.k(kc, None, j * 128, j * 128 + M) + act.k(kc, None, c0, c0 + cw), pk(b), start=(kc == 0), stop=(kc == nkc - 1))
                    epi(f0 // 128 + j, M, c0, cw, b)

    def epi_resid(ft, M, c0, cw, b):
        DVE('tensor_tensor', (xT.a(ft, c0, c0 + cw), pss[b][:, 0:cw], xT.a(ft, c0, c0 + cw), ALU.add),
            pk(b) + xT.k(ft, None, c0, c0 + cw), xT.k(ft, None, c0, c0 + cw))

    def ffn(run, l):
        HID = View(arb, scr_b0, 22, NCOL)
        SL_ = [View(arf, scr_f0 + 2048 + i * 512, 1, 512) for i in range(2)]
        ctr = [0]
        for half in range(2):
            ft0 = half * 22
            for s in range(11):
                f0 = (ft0 + 2 * s) * 128
                rg = load_slab(ffn_w_up[l], 16, f0, 256)
                ru = load_slab(ffn_w_up[l], 16, DFF + f0, 256)
                for j in range(2):
                    for (c0, cw) in colchunks(run):
                        bg = nbank()
                        bu = nbank()
                        for (rv, b) in ((rg, bg), (ru, bu)):
                            for kc in range(16):
                                PE('matmul', (pss[b][:, 0:cw], rv.a(kc, j * 128, j * 128 + 128), hT.a(kc, c0, c0 + cw)),
                                   rv.k(kc, None, j * 128, j * 128 + 128) + hT.k(kc, None, c0, c0 + cw), pk(b), start=(kc == 0), stop=(kc == 15))
                        sl = SL_[ctr[0] % 2]
                        ctr[0] += 1
                        ACT('activation', (sl.a(0, 0, cw), pss[bg][:, 0:cw], AF.Silu), pk(bg), sl.k(0, None, 0, cw))
                        hi = 2 * s + j
                        DVE('tensor_tensor', (HID.a(hi, c0, c0 + cw), sl.a(0, 0, cw), pss[bu][:, 0:cw], ALU.mult),
                            pk(bu) + sl.k(0, None, 0, cw), HID.k(hi, None, c0, c0 + cw))
            for i in range(8):
                Wd = ffn_w_down[l]
                r1 = load_slab(Wd[ft0 * 128:(ft0 + 16) * 128, :], 16, i * 256, 256)
                r2 = load_slab(Wd[(ft0 + 16) * 128:(ft0 + 22) * 128, :], 6, i * 256, 256)
                for j in range(2):
                    for (c0, cw) in colchunks(run):
                        b = nbank()
                        for kc in range(22):
                            rv, kk = (r1, kc) if kc < 16 else (r2, kc - 16)
                            PE('matmul', (pss[b][:, 0:cw], rv.a(kk, j * 128, j * 128 + 128), HID.a(kc, c0, c0 + cw)),
                               rv.k(kk, None, j * 128, j * 128 + 128) + HID.k(kc, None, c0, c0 + cw), pk(b), start=(kc == 0), stop=(kc == 21))
                        epi_resid(i * 2 + j, 128, c0, cw, b)

    def even_mixer(run, e):
        nt = 4 + (1 if run == 0 else 0)
        f = [scr_f0 + 1536]

        def fal(d1, d2):
            v = View(arf, f[0], d1, d2)
            f[0] += d1 * d2
            assert f[0] <= AF_N, f[0]
            return v
        bo = [scr_b0]

        def bal(d1, d2):
            v = View(arb, bo[0], d1, d2)
            bo[0] += d1 * d2
            assert bo[0] <= AB_N, bo[0]
            return v
        AT = fal(8, 544)
        AS = fal(8, NSMP)
        UT = fal(8, NCOL)
        VT = fal(2, 1088)
        T1 = View(arf, VT.off, 2, 544)
        T2 = View(arf, VT.off + 1088, 2, 544)
        MV = fal(1, 16)
        TMPA = fal(1, 256)
        PB = bal(8, NCOL)
        VN = bal(5, 1024)
        PW = bal(8, 256)
        WMT = bal(4, 128)
        DG = bal(4, 16)
        LNG = fal(1, 1024)
        LNB = fal(1, 1024)
        HS = fal(8, NSMP * 16) if run == 0 else None
        WST = bal(4, 128)
        DMA('sp', LNG.a(0), lnrep[e, 0], [], LNG.kall())
        DMA('sp', LNB.a(0), lnrep[e, 1], [], LNB.kall())
        DMA('pool', WST.a3(0, 4), wsT[e].rearrange("p (a b) -> p a b", b=128), [], WST.kall())
        DMA('pool', PW.a3(0, 8), poolw[e].rearrange("p (a b) -> p a b", b=256), [], PW.kall())
        for hh in range(4):
            DVE('tensor_tensor', (WMT.a(hh), WST.a(hh), CS(C_UT), ALU.mult), WST.k(hh) + CK, WMT.k(hh))
        if run == 0:
            DMA('sp', HS.a3(0, 8), spool[e].rearrange("p (a b) -> p a b", b=NSMP * 16), [], HS.kall())
            for hh in range(4):
                DVE('tensor_scalar', (DG.a(hh, 0, 16, 0, 16), CS(C_ID, 16, 16), P("w00_%d" % e, hh)[0:16, :], None, ALU.mult), CK + PK, DG.k(hh))
            DVE('memset', (AT.a3(0, 8, 0, 15), 0.0), [], AT.k(0, 8, 0, 15))
        else:
            DVE('tensor_copy', (AT.a3(0, 8, 0, 15), PH.a(e).rearrange("p (a b) -> p a b", b=15)), PH.k(e), AT.k(0, 8, 0, 15))

        def epi_in(ft, M, c0, cw, b):
            if ft < 8:
                if c0 == 0:
                    ACT('copy', (AT.a(ft, 15, 15 + cw), pss[b][:, 0:cw]), pk(b), AT.k(ft, None, 15, 15 + cw))
                else:
                    ACT('copy', (AS.a(ft), pss[b][:, 0:cw]), pk(b), AS.k(ft))
            else:
                ACT('activation', (UT.a(ft - 8, c0, c0 + cw), pss[b][:, 0:cw], AF.Gelu), pk(b), UT.k(ft - 8, None, c0, c0 + cw))
        stream_fm(run, ev_w_in[e], 16, [(i * 256, 256) for i in range(8)], hT, epi_in)
        rvs = [load_slab(ev_w_in[e], 16, 2048 + s4 * 256, 256) for s4 in range(4)]
        for tt in range(nt):
            c0, M = (tt * 128, 128) if tt < 4 else (T, NSMP)
            vt = tt % 2
            for s4 in range(4):
                rv = rvs[s4]
                b = nbank()
                for kc in range(16):
                    PE('matmul', (pss[b][0:M, 0:256], hT.a(kc, c0, c0 + M), rv.a(kc, 0, 256)),
                       rv.k(kc) + hT.k(kc, None, c0, c0 + M), pk(b), start=(kc == 0), stop=(kc == 15))
                ACT('activation', (VT.a(vt, s4 * 256, s4 * 256 + 256, 0, M), pss[b][0:M, 0:256], AF.Gelu), pk(b), VT.k(vt, None, s4 * 256, s4 * 256 + 256))
            for hf in range(2):
                DVE('bn_stats', (MV.a(0, hf * 6, hf * 6 + 6, 0, M), VT.a(vt, hf * 512, hf * 512 + 512, 0, M)), VT.k(vt), MV.kall())
            DVE('bn_aggr', (MV.a(0, 12, 14, 0, M), MV.a(0, 0, 12, 0, M)), MV.kall(), MV.kall())
            ACT('activation', (MV.a(0, 14, 15, 0, M), MV.a(0, 13, 14, 0, M), AF.Sqrt), MV.kall() + PK, MV.kall(), bias=P("eps")[0:M, :], scale=1.0)
            DVE('reciprocal', (MV.a(0, 15, 16, 0, M), MV.a(0, 14, 15, 0, M)), MV.kall(), MV.kall())
            DVE('tensor_scalar', (VT.a(vt, 0, 1024, 0, M), VT.a(vt, 0, 1024, 0, M), MV.a(0, 12, 13, 0, M), MV.a(0, 15, 16, 0, M), ALU.subtract, ALU.mult),
                VT.k(vt) + MV.kall(), VT.k(vt))
            DVE('tensor_tensor', (VT.a(vt, 0, 1024, 0, M), VT.a(vt, 0, 1024, 0, M), LNG.a(0, 0, 1024, 0, M), ALU.mult), VT.k(vt) + LNG.kall(), VT.k(vt))
            DVE('tensor_tensor', (VT.a(vt, 0, 1024, 0, M), VT.a(vt, 0, 1024, 0, M), LNB.a(0, 0, 1024, 0, M), ALU.add), VT.k(vt) + LNB.kall(), VT.k(vt))
            ACT('copy', (VN.a(tt, 0, 1024, 0, M), VT.a(vt, 0, 1024, 0, M)), VT.k(vt), VN.k(tt))
            if tt == 4:
                DMA('sp', o_v_s[e], VT.a(vt, 0, 1024, 0, NSMP), VT.k(vt), [])
        L = 15 + T
        if run == 0:
            hs4a = HS.a3(0, 8).rearrange("p a (s r) -> p a s r", r=16)
            ACT('copy', (hs4a[:, :, :, 15:16], AS.a3(0, 8).rearrange("p a (s r) -> p a s r", r=1)), AS.kall(), HS.kall())
        for g, w in enumerate((2, 4, 8, 16)):
            c0g = 2 * g
            src = AT
            si = c0g
            bufs = [T1, T2]
            sh = 1
            lvl = 0
            while sh < w:
                dst = bufs[lvl % 2]
                lo = 2 * sh - 1
                DVE('tensor_tensor', (dst.a3(0, 2, lo, L), src.a3(si, si + 2, lo, L), src.a3(si, si + 2, lo - sh, L - sh), ALU.add),
                    src.k(si, si + 2), dst.kall())
                src, si = dst, 0
                sh *= 2
                lvl += 1
            DVE('scalar_tensor_tensor', (PB.a3(c0g, c0g + 2, 0, T), src.a3(0, 2, 15, L), 1.0 / w, AT.a3(c0g, c0g + 2, 15, L), ALU.mult, ALU.subtract),
                src.kall() + AT.k(c0g, c0g + 2), PB.k(c0g, c0g + 2, 0, T))
            if run == 0:
                iv = CST.a(0, C_INV + g * 32, C_INV + g * 32 + 32).rearrange("p (a b) -> p a b", b=16)
                tm = TMPA.a(0, 0, 32).rearrange("p (a b) -> p a b", b=16)
                DVE('tensor_tensor', (tm, src.a3(0, 2, 15, 31), iv, ALU.mult), src.kall() + CK, TMPA.kall())
                DVE('tensor_tensor', (PB.a3(c0g, c0g + 2, 0, 16), tm, AT.a3(c0g, c0g + 2, 15, 31), ALU.subtract),
                    TMPA.kall() + AT.k(c0g, c0g + 2), PB.k(c0g, c0g + 2, 0, 16))
                hs4 = HS.a3(c0g, c0g + 2).rearrange("p a (s r) -> p a s r", r=16)[:, :, :, 16 - w:16]
                tm2 = TMPA.a(0, 64, 96).rearrange("p (a b) -> p a b", b=16)
                DVE('tensor_reduce', (tm2, hs4, AX.X, ALU.add), HS.k(c0g, c0g + 2), TMPA.kall())
                DVE('scalar_tensor_tensor', (PB.a3(c0g, c0g + 2, T, NCOL), tm2, 1.0 / w, AS.a3(c0g, c0g + 2), ALU.mult, ALU.subtract),
                    TMPA.kall() + AS.k(c0g, c0g + 2), PB.k(c0g, c0g + 2, T, NCOL))
        DVE('tensor_copy', (PH.a(e).rearrange("p (a b) -> p a b", b=15), AT.a3(0, 8, T, T + 15)), AT.kall(), PH.k(e))
        if run == NRUN - 1:
            DMA('sp', o_pool_p[e], PH.a(e), PH.k(e), [])
        if run == 0:
            DMA('sp', o_pool_s[e], HS.a3(0, 8).rearrange("p a b -> p (a b)"), HS.kall(), [])
        MO = hT
        for g in range(4):
            for dt in range(2):
                for (c0, cw) in colchunks(run):
                    b = nbank()
                    for k2 in range(2):
                        PE('matmul', (pss[b][:, 0:cw], PW.a(g * 2 + k2, dt * 128, dt * 128 + 128), PB.a(2 * g + k2, c0, c0 + cw)),
                           PW.k(g * 2 + k2) + PB.k(2 * g + k2, None, c0, c0 + cw), pk(b), start=(k2 == 0), stop=(k2 == 1))
                    ch = 2 * g + dt
                    DVE('tensor_scalar', (MO.a(ch, c0, c0 + cw), pss[b][:, 0:cw], P("pscale%d" % e, ch), None, ALU.mult), pk(b) + PK, MO.k(ch, None, c0, c0 + cw))
        for tt in range(nt):
            for hp in range(2):
                b = nbank()
                for q in range(4):
                    ch = hp * 4 + q
                    hh = ch // 2
                    if tt < 4:
                        PE('matmul', (pss[b][:, q * 128:q * 128 + 128], VN.a(tt, ch * 128, ch * 128 + 128), WMT.a(hh)),
                           VN.k(tt, None, ch * 128, ch * 128 + 128) + WMT.k(hh), pk(b), start=True, stop=True)
                    else:
                        PE('matmul', (pss[b][:, q * 128:q * 128 + 16], VN.a(4, ch * 128, ch * 128 + 128, 0, 16), DG.a(hh, 0, 16, 0, 16)),
                           VN.k(4, None, ch * 128, ch * 128 + 128) + DG.k(hh), pk(b), start=True, stop=True)
                for h2 in range(2):
                    hh = hp * 2 + h2
                    ch0 = hh * 2
                    if tt < 4:
                        tm = TMPA.a(0, 0, 256).rearrange("p (a b) -> p a b", b=128)
                        bsr = P("bsrep%d" % e, hh * 128, 128)
                        for c2 in range(2):
                            q0 = h2 * 256 + c2 * 128
                            DVE('tensor_tensor', (TMPA.a(0, c2 * 128, c2 * 128 + 128), pss[b][:, q0:q0 + 128], bsr, ALU.add), pk(b) + PK, TMPA.kall())
                        DVE('tensor_tensor', (MO.a3(8 + ch0, 8 + ch0 + 2, tt * 128, tt * 128 + 128), tm, UT.a3(ch0, ch0 + 2, tt * 128, tt * 128 + 128), ALU.mult),
                            TMPA.kall() + UT.k(ch0, ch0 + 2, tt * 128, tt * 128 + 128), MO.k(8 + ch0, 8 + ch0 + 2, tt * 128, tt * 128 + 128))
                    else:
                        tm = TMPA.a(0, 0, 32).rearrange("p (a b) -> p a b", b=16)
                        pv = pss[b][:, h2 * 256:h2 * 256 + 256].rearrange("p (a b) -> p a b", b=128)[:, :, 0:16]
                        DVE('tensor_scalar', (tm, pv, P("bs0_%d" % e, hh), None, ALU.add), pk(b) + PK, TMPA.kall())
                        DVE('tensor_tensor', (MO.a3(8 + ch0, 8 + ch0 + 2, T, NCOL), tm, UT.a3(ch0, ch0 + 2, T, NCOL), ALU.mult),
                            TMPA.kall() + UT.k(ch0, ch0 + 2, T, NCOL), MO.k(8 + ch0, 8 + ch0 + 2, T, NCOL))
        stream_fm(run, ev_w_out[e], 16, [(i * 256, 256) for i in range(8)], MO, epi_resid)

    def odd_mixer(run, o):
        nt = 4 + (1 if run == 0 else 0)
        f = [scr_f0 + 1536]

        def fal(d1, d2):
            v = View(arf, f[0], d1, d2)
            f[0] += d1 * d2
            assert f[0] <= AF_N, f[0]
            return v
        ABT = fal(5, 16)
        GG = fal(5, 8)
        BB = fal(5, 8)
        SM = fal(1, 64)
        dn0 = f[0]
        OTs = View(arf, dn0 + 8192, 8, NCOL)
        GT = fal(8, 560)
        GS = fal(8, NSMP)
        YT = fal(8, NCOL)
        SIG = fal(2, 512)
        ct0 = f[0] - 1024
        MEAN = fal(1, 512)
        MSQ = fal(1, 512)
        RSTD = fal(1, 512)
        HS2 = fal(8, NSMP * 31) if run == 0 else None
        QKV = View(arb, scr_b0, 24, NCOL)
        ZG = View(arb, scr_b0 + 24 * NCOL, 8, NCOL)
        MO = hT
        CW = lambda kc, j0=0, n=31: P("cw%d" % o, kc * 31 + j0, n)
        rv = load_slab(od_w_in[o], 16, 6144, 16)
        for tt in range(nt):
            c0, M = (tt * 128, 128) if tt < 4 else (T, NSMP)
            b = nbank()
            for kc in range(16):
                PE('matmul', (pss[b][0:M, 0:16], hT.a(kc, c0, c0 + M), rv.a(kc, 0, 16)), rv.k(kc, None, 0, 16) + hT.k(kc, None, c0, c0 + M), pk(b), start=(kc == 0), stop=(kc == 15))
            ACT('copy', (ABT.a(tt, 0, 16, 0, M), pss[b][0:M, 0:16]), pk(b), ABT.k(tt))
        NEA = SM.a(0, 0, 8)
        ACT('activation', (NEA, P("alog%d" % o, 0, 8), AF.Exp), PK, SM.kall())
        DVE('tensor_scalar', (NEA, NEA, -1.0, None, ALU.mult), SM.kall(), SM.kall())
        for tt in range(nt):
            M = 128 if tt < 4 else NSMP
            x_ = SM.a(0, 8, 16, 0, M)
            ax = SM.a(0, 16, 24, 0, M)
            lg = SM.a(0, 24, 32, 0, M)
            DVE('tensor_tensor', (x_, ABT.a(tt, 0, 8, 0, M), P("dtb%d" % o, 0, 8)[0:M, :], ALU.add), ABT.k(tt) + PK, SM.kall())
            ACT('activation', (ax, x_, AF.Abs), SM.kall(), SM.kall())
            ACT('activation', (ax, ax, AF.Exp), SM.kall(), SM.kall(), scale=-1.0)
            ACT('activation', (lg, ax, AF.Ln), SM.kall() + CK, SM.kall(), bias=CS(C_ONE, 1, M), scale=1.0)
            DVE('tensor_scalar', (x_, x_, 0.0, None, ALU.max), SM.kall(), SM.kall())
            DVE('tensor_tensor', (x_, x_, lg, ALU.add), SM.kall(), SM.kall())
            DVE('tensor_tensor', (GG.a(tt, 0, 8, 0, M), x_, SM.a(0, 0, 8, 0, M), ALU.mult), SM.kall(), GG.k(tt))
            ACT('activation', (BB.a(tt, 0, 8, 0, M), ABT.a(tt, 8, 16, 0, M), AF.Sigmoid), ABT.k(tt), BB.k(tt))
        if run == 0:
            DVE('memset', (GT.a3(0, 8, 0, 30), 0.0), [], GT.k(0, 8, 0, 30))
            DMA('sp', HS2.a3(0, 8), sconv[o].rearrange("p (a b) -> p a b", b=NSMP * 31), [], HS2.kall())
        else:
            DVE('tensor_copy', (GT.a3(0, 8, 0, 30), CH.a(o).rearrange("p (a b) -> p a b", b=30)), CH.k(o), GT.k(0, 8, 0, 30))
        sctr = [0]

        def epi_c(ft, M, c0, cw, b):
            if ft < 8:
                if c0 == 0:
                    ACT('copy', (GT.a(ft, 30, 30 + cw), pss[b][:, 0:cw]), pk(b), GT.k(ft, None, 30, 30 + cw))
                else:
                    ACT('copy', (GS.a(ft), pss[b][:, 0:cw]), pk(b), GS.k(ft))
            else:
                sg = SIG.a(sctr[0] % 2, 0, cw)
                sgk = SIG.k(sctr[0] % 2)
                sctr[0] += 1
                ACT('activation', (sg, pss[b][:, 0:cw], AF.Sigmoid), pk(b), sgk)
                if c0 == 0:
                    DVE('tensor_tensor', (GT.a(ft - 8, 30, 30 + cw), GT.a(ft - 8, 30, 30 + cw), sg, ALU.mult), sgk + GT.k(ft - 8, None, 30, 30 + cw), GT.k(ft - 8, None, 30, 30 + cw))
                else:
                    DVE('tensor_tensor', (GS.a(ft - 8), GS.a(ft - 8), sg, ALU.mult), sgk + GS.k(ft - 8), GS.k(ft - 8))
        stream_fm(run, od_w_in[o], 16, [(i * 256, 256) for i in range(8)], hT, epi_c)
        DVE('tensor_copy', (CH.a(o).rearrange("p (a b) -> p a b", b=30), GT.a3(0, 8, T, T + 30)), GT.kall(), CH.k(o))
        if run == NRUN - 1:
            DMA('sp', o_conv_p[o], CH.a(o), CH.k(o), [])
        def conv_taps():
            for kc in range(8):
                DVE('tensor_scalar', (YT.a(kc, 0, T), GT.a(kc, 0, T), CW(kc, 0, 1), P("cdb%d" % o, kc), ALU.mult, ALU.add), GT.k(kc) + PK, YT.k(kc, None, 0, T))
            yield
            for j in range(1, 31):
                for kc in range(8):
                    DVE('scalar_tensor_tensor', (YT.a(kc, 0, T), GT.a(kc, j, j + T), CW(kc, j, 1), YT.a(kc, 0, T), ALU.mult, ALU.add), GT.k(kc) + PK + YT.k(kc, None, 0, T), YT.k(kc, None, 0, T))
                yield
        conv_gen = conv_taps()

        def conv_advance(nsteps=2):
            for _ in range(nsteps):
                try:
                    next(conv_gen)
                except StopIteration:
                    return
        if run == 0:
            h4 = HS2.a3(0, 8).rearrange("p a (s r) -> p a s r", r=31)
            ACT('copy', (h4[:, :, :, 30:31], GS.a3(0, 8).rearrange("p a (s r) -> p a s r", r=1)), GS.kall(), HS2.kall())
            DMA('sp', o_conv_s[o], HS2.a3(0, 8).rearrange("p a b -> p (a b)"), HS2.kall(), [])
            cw3 = P("cw%d" % o, 0, 8 * 31).rearrange("p (a b) -> p a b", b=31)
            for s in range(NSMP):
                DVE('tensor_tensor', (h4[:, :, s, :], h4[:, :, s, :], cw3, ALU.mult), HS2.kall() + PK, HS2.kall())
            DVE('tensor_reduce', (YT.a3(0, 8, T, NCOL), h4, AX.X, ALU.add), HS2.kall(), YT.k(0, 8, T, NCOL))
            for kc in range(8):
                DVE('tensor_scalar', (YT.a(kc, T, NCOL), YT.a(kc, T, NCOL), P("cdb%d" % o, kc), None, ALU.add), YT.k(kc, None, T, NCOL) + PK, YT.k(kc, None, T, NCOL))
        f[0] = ct0
        assert ct0 == dn0 + 8832
        PRE = fal(2, 544)
        CV = fal(2, 512)
        f[0] = dn0 + 12416
        SQ2 = fal(2, 512)
        RS2 = fal(1, 512)
        QS = fal(24, NSMP) if run == 0 else None
        HS3 = fal(24, NSMP * 4) if run == 0 else None
        if run == 0:
            DMA('sp', HS3.a3(0, 24), sdnc[o].rearrange("p (a b) -> p a b", b=NSMP * 4), [], HS3.kall())
            DVE('memset', (DH.a(o), 0.0), [], DH.k(o))
        DWp = lambda ft, j: P("dw%d" % o, ft * 4 + j)
        pctr = [0]

        def epi_q(ft, M, c0, cw, b):
            fq = ft - 16
            if fq >= 24:
                ACT('activation', (ZG.a(fq - 24, c0, c0 + cw), pss[b][:, 0:cw], AF.Silu), pk(b), ZG.k(fq - 24, None, c0, c0 + cw))
                return
            i2 = pctr[0] % 2
            pctr[0] += 1
            cv, cvk = CV.a(i2, 0, cw), CV.k(i2)
            if c0 == 0:
                pre, prk = PRE, PRE.k(i2)
                dh = DH.a(o, fq * 3, fq * 3 + 3)
                DVE('tensor_copy', (PRE.a(i2, 0, 3), dh), DH.k(o), prk)
                ACT('copy', (PRE.a(i2, 3, 3 + cw), pss[b][:, 0:cw]), pk(b), prk)
                DVE('tensor_copy', (dh, PRE.a(i2, cw, cw + 3)), prk, DH.k(o))
                DVE('tensor_scalar', (cv, PRE.a(i2, 0, cw), DWp(fq, 0), None, ALU.mult), prk + PK, cvk)
                for j in range(1, 4):
                    DVE('scalar_tensor_tensor', (cv, PRE.a(i2, j, j + cw), DWp(fq, j), cv, ALU.mult, ALU.add), prk + PK + cvk, cvk)
            else:
                h4 = HS3.a(fq).rearrange("p (s r) -> p s r", r=4)
                ACT('copy', (h4[:, :, 3:4], pss[b][:, 0:cw].rearrange("p (s r) -> p s r", r=1)), pk(b), HS3.k(fq))
                cv3 = cv.rearrange("p (s r) -> p s r", r=1)
                DVE('tensor_scalar', (cv3, h4[:, :, 0:1], DWp(fq, 0), None, ALU.mult), HS3.k(fq) + PK, cvk)
                for j in range(1, 4):
                    DVE('scalar_tensor_tensor', (cv3, h4[:, :, j:j + 1], DWp(fq, j), cv3, ALU.mult, ALU.add), HS3.k(fq) + PK + cvk, cvk)
            ACT('activation', (cv, cv, AF.Silu), cvk, cvk)
            if fq < 16:
                s2, s2k = SQ2.a(i2, 0, cw), SQ2.k(i2)
                ACT('activation', (s2, cv, AF.Square), cvk, s2k)
                bn = nbank()
                PE('matmul', (pss[bn][:, 0:cw], CS(C_ONE), s2), s2k + CK, pk(bn), start=True, stop=True)
                r2 = RS2.a(0, 0, cw)
                ACT('activation', (r2, pss[bn][:, 0:cw], AF.Sqrt), pk(bn) + PK, RS2.kall(), bias=P("eps"), scale=1.0)
                DVE('reciprocal', (r2, r2), RS2.kall(), RS2.kall())
                sc = 128.0 ** -0.5 if fq < 8 else 1.0
                DVE('scalar_tensor_tensor', (QKV.a(fq, c0, c0 + cw), cv, sc, r2, ALU.mult, ALU.mult), cvk + RS2.kall(), QKV.k(fq, None, c0, c0 + cw))
                if c0 != 0:
                    DVE('scalar_tensor_tensor', (QS.a(fq), cv, sc, r2, ALU.mult, ALU.mult), cvk + RS2.kall(), QS.k(fq))
            else:
                ACT('copy', (QKV.a(fq, c0, c0 + cw), cv), cvk, QKV.k(fq, None, c0, c0 + cw))
                if c0 != 0:
                    ACT('copy', (QS.a(fq), cv), cvk, QS.k(fq))
        stream_fm(run, od_w_in[o], 16, [(2048 + i * 256, 256) for i in range(16)], hT, epi_q, after_slab=conv_advance)
        conv_advance(100)
        for (c0, cw) in colchunks(run):
            b1 = nbank()
            b2 = nbank()
            for kc in range(8):
                sq = SIG.a(kc % 2, 0, cw)
                PE('matmul', (pss[b1][:, 0:cw], CS(C_ONE), YT.a(kc, c0, c0 + cw)), YT.k(kc, None, c0, c0 + cw) + CK, pk(b1), start=(kc == 0), stop=(kc == 7))
                ACT('activation', (sq, YT.a(kc, c0, c0 + cw), AF.Square), YT.k(kc, None, c0, c0 + cw), SIG.k(kc % 2))
                PE('matmul', (pss[b2][:, 0:cw], CS(C_ONE), sq), SIG.k(kc % 2) + CK, pk(b2), start=(kc == 0), stop=(kc == 7))
            mn, ms, rs = MEAN.a(0, 0, cw), MSQ.a(0, 0, cw), RSTD.a(0, 0, cw)
            ACT('activation', (mn, pss[b1][:, 0:cw], AF.Copy), pk(b1), MEAN.kall(), scale=1.0 / 1024)
            DVE('tensor_tensor', (ms, mn, mn, ALU.mult), MEAN.kall(), MSQ.kall())
            DVE('scalar_tensor_tensor', (ms, pss[b2][:, 0:cw], 1.0 / 1024, ms, ALU.mult, ALU.subtract), pk(b2) + MSQ.kall(), MSQ.kall())
            ACT('activation', (rs, ms, AF.Sqrt), MSQ.kall() + PK, RSTD.kall(), bias=P("eps"), scale=1.0)
            DVE('reciprocal', (rs, rs), RSTD.kall(), RSTD.kall())
            for kc in range(8):
                yk = YT.k(kc, None, c0, c0 + cw)
                DVE('tensor_tensor', (YT.a(kc, c0, c0 + cw), YT.a(kc, c0, c0 + cw), mn, ALU.subtract), yk + MEAN.kall(), yk)
                DVE('tensor_tensor', (YT.a(kc, c0, c0 + cw), YT.a(kc, c0, c0 + cw), rs, ALU.mult), yk + RSTD.kall(), yk)
        if run == NRUN - 1:
            DMA('sp', o_dnc_p[o], DH.a(o), DH.k(o), [])
        if run == 0:
            DMA('sp', o_dnc_s[o], HS3.a3(0, 24).rearrange("p a b -> p (a b)"), HS3.kall(), [])
            DVE('memset', (SST.a(o), 0.0), [], SST.k(o))
        for (c0, cw) in colchunks(run):
            for kc in range(8):
                ACT('activation', (MO.a(kc, c0, c0 + cw), YT.a(kc, c0, c0 + cw), AF.Silu), YT.k(kc, None, c0, c0 + cw) + PK, MO.k(kc, None, c0, c0 + cw),
                    bias=P("clb%d" % o, kc), scale=P("clg%d" % o, kc))
        f[0] = dn0
        GC, GL, EG, KDF, EGL, NBt, BEG = [View(arf, SM.off + 32 + 0 * 8, 1, 8)] * 7
        f[0] = dn0
        smalls = [fal(1, 8) for _ in range(7)]
        GC, GL, EG, KDF, EGL, NBt, BEG = smalls
        f[0] = dn0 + 64
        tsets = [[fal(1, 128) for _ in range(15)] for _ in range(4)]
        assert f[0] <= dn0 + 8192, f[0]
        IDF = CS(C_ID)

        def dn_head(n, h, TS_, bk_):
            cs, ce = n * 128, n * 128 + 128
            tA, tB, tC, tD, M1, N1, QKM, MA, MB, NA, NBb, QA, QB, U_, WT = TS_
            VNW = MA
            ps_ = lambda s_: pss[bk_][:, s_ * 128:s_ * 128 + 128]
            qn, qnk = QKV.a(h, cs, ce), QKV.k(h, None, cs, ce)
            kn, knk = QKV.a(8 + h, cs, ce), QKV.k(8 + h, None, cs, ce)
            vv, vvk = QKV.a(16 + h, cs, ce), QKV.k(16 + h, None, cs, ce)
            Sv = SST.a(o, h * 128, h * 128 + 128)
            Sk = SST.k(o, None, h * 128, h * 128 + 128)
            DVE('tensor_scalar', (tA.a(0), CS(C_ONE), GG.a(n, h, h + 1), None, ALU.mult), GG.k(n) + CK, tA.kall())
            yield
            PE('matmul', (ps_(0), tA.a(0), CS(C_UT)), tA.kall() + CK, pks(bk_, 0), start=True, stop=True)
            PE('matmul', (ps_(1), kn, kn), knk, pks(bk_, 1), start=True, stop=True)
            PE('matmul', (ps_(2), kn, qn), knk + qnk, pks(bk_, 2), start=True, stop=True)
            gcc = GC.a(0, h, h + 1)
            DVE('tensor_scalar', (tB.a(0), ps_(0), gcc, 0.0, ALU.subtract, ALU.max), pks(bk_, 0) + GC.kall(), tB.kall())
            ACT('activation', (tB.a(0), tB.a(0), AF.Exp), tB.kall(), tB.kall(), scale=-1.0)
            DVE('tensor_scalar', (tC.a(0), ps_(0), gcc, 0.0, ALU.subtract, ALU.min), pks(bk_, 0) + GC.kall(), tC.kall())
            ACT('activation', (tC.a(0), tC.a(0), AF.Exp), tC.kall(), tC.kall())
            ACT('activation', (tD.a(0), ps_(0), AF.Exp), pks(bk_, 0), tD.kall())
            DVE('tensor_tensor', (tA.a(0), ps_(1), tB.a(0), ALU.mult), pks(bk_, 1) + tB.kall(), tA.kall())
            DVE('scalar_tensor_tensor', (M1.a(0), tA.a(0), NBt.a(0, h, h + 1), CS(C_SL), ALU.mult, ALU.mult), tA.kall() + NBt.kall() + CK, M1.kall())
            yield
            PE('transpose', (ps_(3), M1.a(0), IDF), M1.kall() + CK, pks(bk_, 3))
            ACT('copy', (N1.a(0), ps_(3)), pks(bk_, 3), N1.kall())
            DVE('tensor_tensor', (tC.a(0), ps_(2), tC.a(0), ALU.mult), pks(bk_, 2) + tC.kall(), tC.kall())
            DVE('tensor_tensor', (QKM.a(0), tC.a(0), CS(C_UT), ALU.mult), tC.kall() + CK, QKM.kall())
            DVE('tensor_tensor', (QA.a(0), N1.a(0), IDF, ALU.add), N1.kall() + CK, QA.kall())
            PE('transpose', (psb[:, 0:128], kn, IDB.a(0)), knk + IDB.kall(), [('psb', 0)])
            PE('transpose', (psb[:, 128:256], vv, IDB.a(0)), vvk + IDB.kall(), [('psb', 0)])
            DVE('tensor_scalar', (tA.a(0), psb[:, 128:256], BB.a(n, h, h + 1), None, ALU.mult), [('psb', 0)] + BB.k(n), tA.kall())
            DVE('tensor_scalar', (tB.a(0), psb[:, 0:128], BEG.a(0, h, h + 1), None, ALU.mult), [('psb', 0)] + BEG.kall(), tB.kall())
            DVE('tensor_scalar', (tC.a(0), psb[:, 0:128], KDF.a(0, h, h + 1), None, ALU.mult), [('psb', 0)] + KDF.kall(), tC.kall())
            DVE('tensor_tensor', (tD.a(0), qn, tD.a(0), ALU.mult), qnk + tD.kall(), tD.kall())
            yield
            Mc, Nc, Qc = M1, N1, QA
            Mo, No, Qo = [MA, MB], [NA, NBb], [QB, QA]
            for lv in range(6):
                Mn, Nn, Qn = Mo[lv % 2], No[lv % 2], Qo[lv % 2]
                PE('matmul', (ps_(0), Nc.a(0), Mc.a(0)), Nc.kall() + Mc.kall(), pks(bk_, 0), start=True, stop=True)
                if lv < 5:
                    PE('matmul', (ps_(1), Mc.a(0), Nc.a(0)), Nc.kall() + Mc.kall(), pks(bk_, 1), start=True, stop=True)
                if bk_ % 2 == 0:
                    ACT('copy', (Mn.a(0), ps_(0)), pks(bk_, 0), Mn.kall())
                    if lv < 5:
                        ACT('copy', (Nn.a(0), ps_(1)), pks(bk_, 1), Nn.kall())
                else:
                    DVE('tensor_copy', (Mn.a(0), ps_(0)), pks(bk_, 0), Mn.kall())
                    if lv < 5:
                        DVE('tensor_copy', (Nn.a(0), ps_(1)), pks(bk_, 1), Nn.kall())
                yield
                PE('matmul', (ps_(2), Mn.a(0), Qc.a(0)), Mn.kall() + Qc.kall(), pks(bk_, 2), start=True, stop=True)
                DVE('tensor_tensor', (Qn.a(0), Qc.a(0), ps_(2), ALU.add), Qc.kall() + pks(bk_, 2), Qn.kall())
                Mc, Nc, Qc = Mn, Nn, Qn
                yield
            PE('matmul', (ps_(0), Qc.a(0), tA.a(0)), Qc.kall() + tA.kall(), pks(bk_, 0), start=True, stop=True)
            PE('matmul', (ps_(1), tB.a(0), Qc.a(0)), Qc.kall() + tB.kall(), pks(bk_, 1), start=True, stop=True)
            ACT('copy', (U_.a(0), ps_(0)), pks(bk_, 0), U_.kall())
            ACT('copy', (WT.a(0), ps_(1)), pks(bk_, 1), WT.kall())
            yield
            PE('matmul', (ps_(2), WT.a(0), Sv), WT.kall() + Sk, pks(bk_, 2), start=True, stop=True)
            DVE('tensor_tensor', (VNW.a(0), U_.a(0), ps_(2), ALU.subtract), U_.kall() + pks(bk_, 2), VNW.kall())
            PE('matmul', (ps_(3), Sv, tD.a(0)), Sk + tD.kall(), pks(bk_, 3), start=True, stop=False)
            yield
            PE('matmul', (ps_(3), VNW.a(0), QKM.a(0)), VNW.kall() + QKM.kall(), pks(bk_, 3), start=False, stop=True)
            ACT('copy', (OTs.a(h, cs, ce), ps_(3)), pks(bk_, 3), OTs.k(h, None, cs, ce))
            PE('matmul', (ps_(0), tC.a(0), VNW.a(0)), tC.kall() + VNW.kall(), pks(bk_, 0), start=True, stop=True)
            DVE('scalar_tensor_tensor', (Sv, Sv, EGL.a(0, h, h + 1), ps_(0), ALU.mult, ALU.add), Sk + EGL.kall() + pks(bk_, 0), Sk)
            yield

        for n in range(4):
            b = 4 + (n % 3)
            PE('matmul', (pss[b][:, 0:8], CS(C_UT), GG.a(n)), GG.k(n) + CK, pks(b, 0), start=True, stop=True)
            PE('matmul', (pss[b][:, 8:16], CS(C_ONE), GG.a(n)), GG.k(n) + CK, pks(b, 0), start=True, stop=True)
            ACT('copy', (GC.a(0), pss[b][:, 0:8]), pks(b, 0), GC.kall())
            ACT('copy', (GL.a(0), pss[b][:, 8:16]), pks(b, 0), GL.kall())
            ACT('activation', (EG.a(0), GC.a(0), AF.Exp), GC.kall(), EG.kall())
            DVE('tensor_tensor', (KDF.a(0), GL.a(0), GC.a(0), ALU.subtract), GL.kall() + GC.kall(), KDF.kall())
            ACT('activation', (KDF.a(0), KDF.a(0), AF.Exp), KDF.kall(), KDF.kall())
            ACT('activation', (EGL.a(0), GL.a(0), AF.Exp), GL.kall(), EGL.kall())
            DVE('tensor_scalar', (NBt.a(0), BB.a(n), -1.0, None, ALU.mult), BB.k(n), NBt.kall())
            DVE('tensor_tensor', (BEG.a(0), BB.a(n), EG.a(0), ALU.mult), BB.k(n) + EG.kall(), BEG.kall())
            for hg in range(2):
                gens = [dn_head(n, hg * 4 + i, tsets[i], i) for i in range(4)]
                if os.environ.get('K_SEQ'):
                    for g_ in gens:
                        for _ in g_:
                            pass
                    gens = []
                alive = list(gens)
                while alive:
                    nxt = []
                    for g_ in alive:
                        try:
                            next(g_)
                            nxt.append(g_)
                        except StopIteration:
                            pass
                    alive = nxt
        bank_ctr[0] = 0
        if run == NRUN - 1:
            DMA('sp', o_S_p[o], SST.a(o), SST.k(o), [])
        if run == 0:
            f[0] = dn0 + 64
            RR = fal(1, 128)
            EGB = fal(1, 128)
            BEB = fal(1, 128)
            SS2 = [fal(8, 128), fal(8, 128)]
            TS = fal(1, 32)
            VREP = fal(1, 128)
            T3 = fal(1, 128)
            assert f[0] <= dn0 + 8192, f[0]
            ACT('activation', (EG.a(0, 0, 8, 0, NSMP), GG.a(4, 0, 8, 0, NSMP), AF.Exp), GG.k(4), EG.kall())
            for (srcv, srck, dst) in ((EG.a(0, 0, 8, 0, NSMP), EG.kall(), EGB), (BB.a(4, 0, 8, 0, NSMP), BB.k(4), BEB)):
                r3 = RR.a(0, 0, 128, 0, NSMP).rearrange("p (s h) -> p s h", h=8)
                for h in range(8):
                    DVE('tensor_scalar', (r3[:, :, h:h + 1], CS(C_ID, 16, 16).rearrange("p (s h) -> p s h", h=1), srcv[:, h:h + 1], None, ALU.mult), srck + CK, RR.kall())
                bq = nbank()
                PE('matmul', (pss[bq][:, 0:128], CS(C_ONE, 128, NSMP), RR.a(0, 0, 128, 0, NSMP)), RR.kall() + CK, pk(bq), start=True, stop=True)
                ACT('copy', (dst.a(0), pss[bq][:, 0:128]), pk(bq), dst.kall())
            for s in range(NSMP):
                SS = SS2[s % 2]
                DMA('sp', SS.a3(0, 8).rearrange("p a b -> p (a b)"), sS[o, s], [], SS.kall())
                bk = nbank()
                for h in range(8):
                    PE('matmul', (pss[bk][:, h:h + 1], SS.a(h), QS.a(8 + h, s, s + 1)), SS.k(h) + QS.k(8 + h), pk(bk), start=True, stop=True)
                t8 = TS.a(0, 0, 8)
                DVE('tensor_tensor', (t8, pss[bk][:, 0:8], EGB.a(0, s * 8, s * 8 + 8), ALU.mult), pk(bk) + EGB.kall(), TS.kall())
                vs8 = QS.a3(16, 24, s, s + 1).rearrange("p a b -> p (a b)")
                DVE('tensor_tensor', (t8, vs8, t8, ALU.subtract), QS.k(16, 24) + TS.kall(), TS.kall())
                DVE('tensor_tensor', (t8, t8, BEB.a(0, s * 8, s * 8 + 8), ALU.mult), BEB.kall() + TS.kall(), TS.kall())
                for h in range(8):
                    DVE('tensor_scalar', (VREP.a(0), CS(C_ONE), TS.a(0, h, h + 1), None, ALU.mult), TS.kall() + CK, VREP.kall())
                    bv = nbank()
                    PE('matmul', (pss[bv][:, 0:128], VREP.a(0), IDF), VREP.kall() + CK, pk(bv), start=True, stop=True)
                    DVE('tensor_scalar', (T3.a(0), SS.a(h), EGB.a(0, s * 8 + h, s * 8 + h + 1), None, ALU.mult), SS.k(h) + EGB.kall(), T3.kall())
                    DVE('scalar_tensor_tensor', (SS.a(h), pss[bv][:, 0:128], QS.a(8 + h, s, s + 1), T3.a(0), ALU.mult, ALU.add), pk(bv) + QS.k(8 + h) + T3.kall(), SS.k(h))
                bo2 = nbank()
                for h in range(8):
                    PE('matmul', (pss[bo2][:, h:h + 1], SS.a(h), QS.a(h, s, s + 1)), SS.k(h) + QS.k(h), pk(bo2), start=True, stop=True)
                ACT('copy', (OTs.a3(0, 8, T + s, T + s + 1).rearrange("p a b -> p (a b)"), pss[bo2][:, 0:8]), pk(bo2), OTs.k(0, 8, T + s, T + s + 1))
                DMA('sp', o_S_s[o, s], SS.a3(0, 8).rearrange("p a b -> p (a b)"), SS.kall(), [])
        for h in range(8):
            for (c0, cw) in colchunks(run):
                ok = OTs.k(h, None, c0, c0 + cw)
                s2 = SQ2.a(0, 0, cw)
                ACT('activation', (s2, OTs.a(h, c0, c0 + cw), AF.Square), ok, SQ2.k(0))
                bn = nbank()
                PE('matmul', (pss[bn][:, 0:cw], CS(C_ONE), s2), SQ2.k(0) + CK, pk(bn), start=True, stop=True)
                r2 = RS2.a(0, 0, cw)
                ACT('activation', (r2, pss[bn][:, 0:cw], AF.Sqrt), pk(bn) + PK, RS2.kall(), bias=P("eps"), scale=1.0 / 128)
                DVE('reciprocal', (r2, r2), RS2.kall(), RS2.kall())
                DVE('scalar_tensor_tensor', (OTs.a(h, c0, c0 + cw), OTs.a(h, c0, c0 + cw), P("ng%d" % o), r2, ALU.mult, ALU.mult), ok + PK + RS2.kall(), ok)
                DVE('tensor_tensor', (MO.a(8 + h, c0, c0 + cw), OTs.a(h, c0, c0 + cw), ZG.a(h, c0, c0 + cw), ALU.mult), ok + ZG.k(h, None, c0, c0 + cw), MO.k(8 + h, None, c0, c0 + cw))
        stream_fm(run, od_w_out[o], 16, [(i * 256, 256) for i in range(8)], MO, epi_resid)

    for run in range(NRUN):
        cur_run[0] = run
        slab_seq[0] = 0
        DMA('sp', xT.a3(0, 16, 0, T), xin[:, :, run * T:(run + 1) * T], [], xT.k(0, 16, 0, T))
        if run == 0:
            DMA('sp', xT.a3(0, 16, T, NCOL), xin[:, :, 2048:2048 + NSMP], [], xT.k(0, 16, T, NCOL))
        for l in range(DBG_LAYERS):
            rmsnorm(run, "nmix%d" % l)
            if l % 2 == 0:
                even_mixer(run, l // 2)
            else:
                odd_mixer(run, l // 2)
            rmsnorm(run, "nffn%d" % l)
            ffn(run, l)
        YO = View(arf, scr_f0 + 1536, 16, NCOL)
        rmsnorm(run, "nfin", out_bf16=False, outv=YO)
        DMA('sp', yout[:, :, run * T:(run + 1) * T], YO.a3(0, 16, 0, T), YO.k(0, 16, 0, T), [])
        if run == 0:
            DMA('sp', yout[:, :, 2048:2048 + NSMP], YO.a3(0, 16, T, NCOL), YO.k(0, 16, T, NCOL), [])
    S.emit(nc, es)
    es.close()
    return nc


def _fm(v, nch):
    sh = v.shape[:-1]
    a = v.reshape(sh + (nch, 128))
    a = np.moveaxis(a, -1, 0)
    return np.ascontiguousarray(a)


_NC_CACHE = {}


def kernel(**inp):
    f32 = np.float32
    g = lambda k: np.asarray(inp[k], dtype=f32)
    x_prompt, x_sample = g('x_prompt'), g('x_sample')
    if 'nc' not in _NC_CACHE:
        _NC_CACHE['nc'] = build_program()
    nc = _NC_CACHE['nc']
    L = PLAYOUT
    prm = np.zeros((128, 2048), f32)

    def put(name, arr):
        arr = np.asarray(arr, f32)
        if arr.ndim == 1:
            arr = np.broadcast_to(arr[None, :], (128, arr.shape[0]))
        prm[:, L[name]:L[name] + arr.shape[1]] = arr
    for l in range(4):
        put("nmix%d" % l, _fm(g('norm_mix')[l], 16))
        put("nffn%d" % l, _fm(g('norm_ffn')[l], 16))
    put("nfin", _fm(g('norm_final'), 16))
    for e in range(2):
        put("pscale%d" % e, _fm(g('pool_scale')[e], 8))
        put("bs0_%d" % e, g('sgu_b')[e][:, 0])
        put("w00_%d" % e, g('sgu_ws')[e][:, 0, 0])
        put("bsrep%d" % e, g('sgu_b')[e].reshape(-1))
    for o in range(2):
        put("cw%d" % o, _fm(g('cv_dw')[o], 8).transpose(0, 2, 1).reshape(128, -1))
        put("cdb%d" % o, _fm(g('cv_db')[o], 8))
        put("clg%d" % o, _fm(g('cv_ln_g')[o], 8))
        put("clb%d" % o, _fm(g('cv_ln_b')[o], 8))
        put("dw%d" % o, _fm(g('dn_conv_w')[o], 24).transpose(0, 2, 1).reshape(128, -1))
        put("ng%d" % o, g('dn_norm_g')[o][:, None])
        put("alog%d" % o, g('dn_a_log')[o])
        put("dtb%d" % o, g('dn_dt_bias')[o])
    put("eps", np.full((1,), EPS, f32))
    lnrep = np.stack([np.stack([np.broadcast_to(g('sgu_ln_g')[e][None], (128, 1024)),
                                np.broadcast_to(g('sgu_ln_b')[e][None], (128, 1024))]) for e in range(2)]).astype(f32)
    pw = g('pool_w')
    poolw = np.ascontiguousarray(pw.reshape(2, 4, 2, 128, 256).transpose(0, 3, 1, 2, 4).reshape(2, 128, -1))
    wsT = np.ascontiguousarray(g('sgu_ws').transpose(0, 3, 1, 2).reshape(2, 128, -1))
    idx = np.arange(128)
    ident = np.eye(128, dtype=f32)
    ut = (idx[:, None] <= idx[None, :]).astype(f32)
    sl = (idx[:, None] > idx[None, :]).astype(f32)
    ones = np.ones((128, 128), f32)
    inv = np.zeros((4, 2, 16), f32)
    for gi, w in enumerate((2, 4, 8, 16)):
        inv[gi, :, :] = 1.0 / np.minimum(w, np.arange(16) + 1)
    cst = np.concatenate([ident, ut, sl, ones, ones, ones, np.broadcast_to(inv.reshape(1, -1), (128, 128))], axis=1).astype(f32)
    cst = np.ascontiguousarray(cst)
    in_maps = []
    for c in range(8):
        b = c // 2
        s0 = c * NSMP
        xin = np.concatenate([_fm(x_prompt[b], 16), _fm(x_sample[s0:s0 + NSMP, 0], 16)], axis=1)
        xin = np.ascontiguousarray(xin.transpose(0, 2, 1))
        sp = _fm(g('state_pool')[:, s0:s0 + NSMP], 8)
        sp = np.concatenate([sp, np.zeros_like(sp[:, :, :, :1])], axis=3)
        sp = np.ascontiguousarray(sp.transpose(1, 0, 4, 2, 3).reshape(2, 128, -1))
        sc = _fm(g('state_conv_c')[:, s0:s0 + NSMP], 8)
        sc = np.concatenate([sc, np.zeros_like(sc[:, :, :, :1])], axis=3)
        sc = np.ascontiguousarray(sc.transpose(1, 0, 4, 2, 3).reshape(2, 128, -1))
        sd = _fm(g('state_dn_conv')[:, s0:s0 + NSMP], 24)
        sd = np.concatenate([sd, np.zeros_like(sd[:, :, :, :1])], axis=3)
        sd = np.ascontiguousarray(sd.transpose(1, 0, 4, 2, 3).reshape(2, 128, -1))
        ss = g('state_dn_S')[:, s0:s0 + NSMP]
        ss = np.ascontiguousarray(ss.transpose(0, 1, 3, 2, 4).reshape(2, NSMP, 128, -1))
        in_maps.append(dict(xin=xin, spool=sp, sconv=sc, sdnc=sd, sS=ss, ev_w_in=g('ev_w_in'), ev_w_out=g('ev_w_out'),
                            od_w_in=g('od_w_in'), od_w_out=g('od_w_out'), ffn_w_up=g('ffn_w_up'), ffn_w_down=g('ffn_w_down'),
                            prm=prm, lnrep=lnrep, poolw=poolw, wsT=wsT, cst=cst))
    res = run_bass_kernel_spmd(nc, in_maps, core_ids=list(range(8)))
    R = res.results

    def unfm(a):
        a = np.moveaxis(a, 0, -1)
        a = np.moveaxis(a, 0, -2)
        return a.reshape(a.shape[:-2] + (-1,))
    y_prompt = np.zeros((4, 2048, D), f32)
    y_sample = np.zeros((128, 1, D), f32)
    pool_p = np.zeros((2, 4, 15, 1024), f32)
    pool_s = np.zeros((2, 128, 15, 1024), f32)
    v_s = np.zeros((2, 128, 1, 1024), f32)
    cc_p = np.zeros((2, 4, 30, 1024), f32)
    cc_s = np.zeros((2, 128, 30, 1024), f32)
    dc_p = np.zeros((2, 4, 3, 3072), f32)
    dc_s = np.zeros((2, 128, 3, 3072), f32)
    S_p = np.zeros((2, 4, 8, 128, 128), f32)
    S_s = np.zeros((2, 128, 8, 128, 128), f32)
    for c in range(8):
        r = R[c]
        b = c // 2
        s0 = c * NSMP
        yo = unfm(r['yout'])
        if c % 2 == 0:
            y_prompt[b] = yo[:2048]
            for e in range(2):
                pool_p[e, b] = unfm(r['o_pool_p'][e].reshape(128, 8, 15))
                cc_p[e, b] = unfm(r['o_conv_p'][e].reshape(128, 8, 30))
                dc_p[e, b] = unfm(r['o_dnc_p'][e].reshape(128, 24, 3))
                S_p[e, b] = r['o_S_p'][e].reshape(128, 8, 128).transpose(1, 0, 2)
        y_sample[s0:s0 + NSMP, 0] = yo[2048:]
        for e in range(2):
            pool_s[e, s0:s0 + NSMP] = unfm(r['o_pool_s'][e].reshape(128, 8, NSMP, 16)[:, :, :, 1:])
            v_s[e, s0:s0 + NSMP, 0] = r['o_v_s'][e]
            cc_s[e, s0:s0 + NSMP] = unfm(r['o_conv_s'][e].reshape(128, 8, NSMP, 31)[:, :, :, 1:])
            dc_s[e, s0:s0 + NSMP] = unfm(r['o_dnc_s'][e].reshape(128, 24, NSMP, 4)[:, :, :, 1:])
            S_s[e, s0:s0 + NSMP] = r['o_S_s'][e].reshape(NSMP, 128, 8, 128).transpose(0, 2, 1, 3)
    return (y_prompt, y_sample, pool_p, pool_s, v_s, cc_p, cc_s, dc_p, dc_s, S_p, S_s)
```

```python
import numpy as np
import concourse.bass as bass
import concourse.mybir as mybir
from concourse.bass_utils import run_bass_kernel_spmd
from contextlib import ExitStack

F32 = mybir.dt.float32
BF16 = mybir.dt.bfloat16
AF = mybir.ActivationFunctionType
ALU = mybir.AluOpType
AX = mybir.AxisListType

D = 2048
DEPTH = 4
T = 512
import os
NRUN = int(os.environ.get('K_NRUN', '4'))
NSMP = 16
NCOL = T + NSMP
DFF = 5632
EPS = 1e-6
RING = 4
PLAYOUT = {}
DBG_LAYERS = int(os.environ.get('K_LAYERS', '4'))


class Sched:
    def __init__(self):
        self.ops = []

    def add(self, eng, meth, args, kwargs, R, W, dma=False):
        self.ops.append([eng, meth, args, kwargs, tuple(R), tuple(W), dma])

    def emit(self, nc, es):
        ops = self.ops
        n = len(ops)
        lastw = {}
        readers = {}
        deps = [None] * n
        needed = [False] * n
        for i, (eng, meth, args, kw, R, W, dma) in enumerate(ops):
            d = set()
            for k in R:
                j = lastw.get(k)
                if j is not None:
                    d.add(j)
                if k[0] == 'ps' or k[0] == 'psb':
                    for j in readers.get(k, ()):
                        if ops[j][0] != eng:
                            d.add(j)
            for k in W:
                j = lastw.get(k)
                if j is not None:
                    d.add(j)
                for j in readers.get(k, ()):
                    d.add(j)
            for k in R:
                readers.setdefault(k, []).append(i)
            for k in W:
                lastw[k] = i
                readers[k] = []
            d.discard(i)
            dd = []
            for j in d:
                if ops[j][0] == 'pe' and eng == 'pe':
                    continue
                dd.append(j)
                needed[j] = True
            deps[i] = dd
        engs = ['pe', 'act', 'dve', 'pool', 'sp']
        NDS = 12
        csem = {e: es.enter_context(nc.semaphore("c_" + e)) for e in engs}
        dsem = {e: [es.enter_context(nc.semaphore("d_%s%d" % (e, i))) for i in range(NDS)] for e in ('pool', 'sp', 'act')}
        ccount = {e: 0 for e in engs}
        dcount = {e: 0 for e in dsem}
        sig = [None] * n
        prevdma = [None] * n
        for i, op in enumerate(ops):
            eng = op[0]
            if op[6]:
                m = dcount[eng]
                dcount[eng] += 1
                s = dsem[eng][m % NDS]
                sig[i] = (s, 16 * (m // NDS + 1), 16)
                if m >= NDS:
                    prevdma[i] = (s, 16 * (m // NDS))
            elif needed[i]:
                ccount[eng] += 1
                sig[i] = (csem[eng], ccount[eng], 1)
        per = {e: [] for e in engs}
        for i, op in enumerate(ops):
            per[op[0]].append(i)
        finals = []
        for e in dsem:
            for k, s in enumerate(dsem[e]):
                cnt = (dcount[e] - k + NDS - 1) // NDS if dcount[e] > k else 0
                if cnt > 0:
                    finals.append((s, 16 * cnt))
        block = es.enter_context(nc.Block())

        def run_engine(engobj, e):
            waited = {}
            for i in per[e]:
                eng, meth, args, kw, R, W, dma = ops[i]
                wl = {}
                for j in deps[i]:
                    s, v, _ = sig[j]
                    key = id(s)
                    if key not in wl or wl[key][1] < v:
                        wl[key] = (s, v)
                if prevdma[i] is not None:
                    s, v = prevdma[i]
                    key = id(s)
                    if key not in wl or wl[key][1] < v:
                        wl[key] = (s, v)
                for key, (s, v) in wl.items():
                    if waited.get(key, 0) >= v:
                        continue
                    engobj.wait_ge(s, v)
                    waited[key] = v
                ins = getattr(engobj, meth)(*args, **kw)
                if sig[i] is not None:
                    ins.then_inc(sig[i][0], sig[i][2])
            if e == 'sp':
                for s, v in finals:
                    engobj.wait_ge(s, v)

        @block.tensor
        def _(eng):
            run_engine(eng, 'pe')

        @block.scalar
        def _(eng):
            run_engine(eng, 'act')

        @block.vector
        def _(eng):
            run_engine(eng, 'dve')

        @block.gpsimd
        def _(eng):
            run_engine(eng, 'pool')

        @block.sync
        def _(eng):
            run_engine(eng, 'sp')


class Arena:
    def __init__(self, nc, es, name, nelem, dtype):
        self.t = es.enter_context(nc.sbuf_tensor(name, [128, nelem], dtype))
        self.name = name
        self.n = nelem
        self.off = 0

    def alloc(self, d1, d2, at=None):
        if at is None:
            at = self.off
            self.off = at + d1 * d2
        assert at + d1 * d2 <= self.n, (self.name, at, d1, d2, self.n)
        return View(self, at, d1, d2)


class View:
    BLK = 64

    def __init__(self, ar, off, d1, d2):
        self.ar, self.off, self.d1, self.d2 = ar, off, d1, d2

    def a(self, i, lo=0, hi=None, p0=0, p1=128):
        hi = self.d2 if hi is None else hi
        b = self.off + i * self.d2
        return self.ar.t[p0:p1, b + lo:b + hi]

    def a3(self, i0, i1, lo=0, hi=None, p0=0, p1=128):
        hi = self.d2 if hi is None else hi
        b = self.off
        v = self.ar.t[p0:p1, b + i0 * self.d2:b + i1 * self.d2].rearrange("p (a b) -> p a b", b=self.d2)
        return v[:, :, lo:hi]

    def k(self, i0, i1=None, lo=0, hi=None):
        hi = self.d2 if hi is None else hi
        i1 = i0 + 1 if i1 is None else i1
        ks = []
        for i in range(i0, i1):
            b = self.off + i * self.d2
            for blk in range((b + lo) // self.BLK, (b + hi - 1) // self.BLK + 1):
                ks.append((self.ar.name, blk))
        return ks

    def kall(self):
        return self.k(0, self.d1)


def build_program():
    nc = bass.Bass('TRN2', target_bir_lowering=False)
    S = Sched()
    es = ExitStack()

    def din(name, shape):
        return nc.dram_tensor(name, list(shape), F32, kind="ExternalInput").ap()

    def dout(name, shape):
        return nc.dram_tensor(name, list(shape), F32, kind="ExternalOutput").ap()

    xin = din("xin", [128, 16, 2048 + NSMP])
    yout = dout("yout", [128, 16, 2048 + NSMP])
    spool = din("spool", [2, 128, 8 * NSMP * 16])
    sconv = din("sconv", [2, 128, 8 * NSMP * 31])
    sdnc = din("sdnc", [2, 128, 24 * NSMP * 4])
    sS = din("sS", [2, NSMP, 128, 8 * 128])
    o_pool_p = dout("o_pool_p", [2, 128, 8 * 15])
    o_pool_s = dout("o_pool_s", [2, 128, 8 * NSMP * 16])
    o_v_s = dout("o_v_s", [2, NSMP, 1024])
    o_conv_p = dout("o_conv_p", [2, 128, 8 * 30])
    o_conv_s = dout("o_conv_s", [2, 128, 8 * NSMP * 31])
    o_dnc_p = dout("o_dnc_p", [2, 128, 24 * 3])
    o_dnc_s = dout("o_dnc_s", [2, 128, 24 * NSMP * 4])
    o_S_p = dout("o_S_p", [2, 128, 8 * 128])
    o_S_s = dout("o_S_s", [2, NSMP, 128, 8 * 128])
    ev_w_in = din("ev_w_in", [2, D, 3072])
    ev_w_out = din("ev_w_out", [2, D, D])
    od_w_in = din("od_w_in", [2, D, 6160])
    od_w_out = din("od_w_out", [2, D, D])
    ffn_w_up = din("ffn_w_up", [4, D, 2 * DFF])
    ffn_w_down = din("ffn_w_down", [4, DFF, D])
    NPF = 16 * 9 + 2 * 8 + 2 * 8 * 31 + 2 * 8 * 3 + 2 * 24 * 4 + 2 + 2 * 8 * 3
    prm = din("prm", [128, 2048])
    lnrep = din("lnrep", [2, 2, 128, 1024])
    poolw = din("poolw", [2, 128, 4 * 2 * 256])
    wsT = din("wsT", [2, 128, 4 * 128])
    cst = din("cst", [128, 6 * 128 + 4 * 2 * 16])

    AF_N = 32200
    AB_N = 16 * NCOL + RING * 16 * 256 + 16896 + 256
    arf = Arena(nc, es, "arf", AF_N, F32)
    arb = Arena(nc, es, "arb", AB_N, BF16)
    xT = arf.alloc(16, NCOL)
    PRM = arf.alloc(1, 2048)
    CST = arf.alloc(1, 6 * 128 + 128)
    PH = arf.alloc(2, 8 * 15)
    CH = arf.alloc(2, 8 * 30)
    DH = arf.alloc(2, 24 * 3)
    SST = arf.alloc(2, 8 * 128)
    scr_f0 = arf.off
    hT = arb.alloc(16, NCOL)
    ringv = [arb.alloc(16, 256) for _ in range(RING)]
    IDB = arb.alloc(1, 128)
    ONEB = arb.alloc(1, 128)
    scr_b0 = arb.off
    pss = [es.enter_context(nc.psum_tensor("ps%d" % i, [128, 512], F32)) for i in range(7)]
    psb = es.enter_context(nc.psum_tensor("psb", [128, 1024], BF16))

    def PE(meth, args, R, W, **kw):
        S.add('pe', meth, args, kw, R, W)

    def ACT(meth, args, R, W, **kw):
        S.add('act', meth, args, kw, R, W)

    def DVE(meth, args, R, W, **kw):
        S.add('dve', meth, args, kw, R, W)

    def DMA(eng, out, in_, R, W):
        S.add(eng, 'dma_start', (), dict(out=out, in_=in_), R, W, dma=True)

    bank_ctr = [0]

    def nbank():
        b = bank_ctr[0] % 7
        bank_ctr[0] += 1
        return b

    def pk(b):
        return [('ps', b, s_) for s_ in range(4)]

    def pks(b, s_):
        return [('ps', b, q_) for q_ in range(4)]

    ring_ctr = [0]

    def load_slab(Wap, nkc, f0, fw):
        r = ring_ctr[0] % RING
        ring_ctr[0] += 1
        rv = ringv[r]
        src = Wap.rearrange("(kc p) f -> p kc f", p=128)[:, 0:nkc, f0:f0 + fw]
        dst = rv.a3(0, nkc, 0, fw)
        DMA('pool', dst, src, [], rv.k(0, nkc, 0, fw))
        return rv

    poff = {}
    pcur = [0]

    def pdef(name, n):
        poff[name] = pcur[0]
        pcur[0] += n

    for l in range(4):
        pdef("nmix%d" % l, 16)
        pdef("nffn%d" % l, 16)
    pdef("nfin", 16)
    for e in range(2):
        pdef("pscale%d" % e, 8)
        pdef("bs0_%d" % e, 4)
        pdef("w00_%d" % e, 4)
        pdef("bsrep%d" % e, 4 * 128)
    for o in range(2):
        pdef("cw%d" % o, 8 * 31)
        pdef("cdb%d" % o, 8)
        pdef("clg%d" % o, 8)
        pdef("clb%d" % o, 8)
        pdef("dw%d" % o, 24 * 4)
        pdef("ng%d" % o, 1)
        pdef("alog%d" % o, 8)
        pdef("dtb%d" % o, 8)
    pdef("eps", 1)
    assert pcur[0] <= 2048, pcur[0]
    PLAYOUT.update(poff)

    def P(name, i=0, n=1):
        b = poff[name] + i
        return PRM.a(0, b, b + n)

    PK = PRM.kall()
    C_ID, C_UT, C_SL, C_ONE = 0, 128, 256, 384
    C_INV = 768

    def CS(off, n=128, p1=128):
        return CST.a(0, off, off + n, 0, p1)

    CK = CST.kall()

    DMA('sp', PRM.a(0), prm[:, :], [], PK)
    DMA('sp', CST.a(0, 0, 6 * 128 + 128), cst[:, :], [], CK)
    DMA('pool', IDB.a(0), cst[:, 0:128], [], IDB.kall())
    DMA('pool', ONEB.a(0), cst[:, C_ONE:C_ONE + 128], [], ONEB.kall())

    def colchunks(run):
        return [(0, T)] + ([(T, NSMP)] if run == 0 else [])

    def rmsnorm(run, gname, out_bf16=True, outv=None):
        fo = scr_f0
        SQ = [View(arb, scr_b0, 1, 512), View(arb, scr_b0 + 512, 1, 512)]
        RS = View(arf, fo + 1024, 1, 512)
        for (c0, cw) in colchunks(run):
            b = nbank()
            for kc in range(16):
                sq = SQ[kc % 2]
                ACT('activation', (sq.a(0, 0, cw), xT.a(kc, c0, c0 + cw), AF.Square), xT.k(kc, None, c0, c0 + cw), sq.k(0, None, 0, cw))
                PE('matmul', (pss[b][:, 0:cw], ONEB.a(0), sq.a(0, 0, cw)), sq.k(0, None, 0, cw) + ONEB.kall(), pk(b), start=(kc == 0), stop=(kc == 15))
            ACT('activation', (RS.a(0, 0, cw), pss[b][:, 0:cw], AF.Sqrt), pk(b) + PK, RS.k(0, None, 0, cw), scale=1.0 / D, bias=P("eps"))
            DVE('reciprocal', (RS.a(0, 0, cw), RS.a(0, 0, cw)), RS.k(0, None, 0, cw), RS.k(0, None, 0, cw))
            for kc in range(16):
                if out_bf16:
                    o_ap, o_k = hT.a(kc, c0, c0 + cw), hT.k(kc, None, c0, c0 + cw)
                else:
                    o_ap, o_k = outv.a(kc, c0, c0 + cw), outv.k(kc, None, c0, c0 + cw)
                DVE('scalar_tensor_tensor', (o_ap, xT.a(kc, c0, c0 + cw), P(gname, kc), RS.a(0, 0, cw), ALU.mult, ALU.mult),
                    xT.k(kc, None, c0, c0 + cw) + RS.k(0, None, 0, cw) + PK, o_k)

    def stream_fm(run, Wap, nkc, slabs, act, epi, after_slab=None):
        for (f0, fw) in slabs:
            if after_slab is not None:
                after_slab()
            rv = load_slab(Wap, nkc, f0, fw)
            for j in range((fw + 127) // 128):
                M = min(128, fw - j * 128)
                for (c0, cw) in colchunks(run):
                    b = nbank()
                    for kc in range(nkc):
                        PE('matmul', (pss[b][0:M, 0:cw], rv.a(kc, j * 128, j * 128 + M), act.a(kc, c0, c0 + cw)),
                           rv.k(kc, None, j * 128, j * 128 + M) + act.k(kc, None, c0, c0 + cw), pk(b), start=(kc == 0), stop=(kc == nkc - 1))
                    epi(f0 // 128 + j, M, c0, cw, b)

    def epi_resid(ft, M, c0, cw, b):
        DVE('tensor_tensor', (xT.a(ft, c0, c0 + cw), pss[b][:, 0:cw], xT.a(ft, c0, c0 + cw), ALU.add),
            pk(b) + xT.k(ft, None, c0, c0 + cw), xT.k(ft, None, c0, c0 + cw))

    def ffn(run, l):
        HID = View(arb, scr_b0, 22, NCOL)
        SL_ = [View(arf, scr_f0 + 2048 + i * 512, 1, 512) for i in range(2)]
        ctr = [0]
        for half in range(2):
            ft0 = half * 22
            for s in range(11):
                f0 = (ft0 + 2 * s) * 128
                rg = load_slab(ffn_w_up[l], 16, f0, 256)
                ru = load_slab(ffn_w_up[l], 16, DFF + f0, 256)
                for j in range(2):
                    for (c0, cw) in colchunks(run):
                        bg = nbank()
                        bu = nbank()
                        for (rv, b) in ((rg, bg), (ru, bu)):
                            for kc in range(16):
                                PE('matmul', (pss[b][:, 0:cw], rv.a(kc, j * 128, j * 128 + 128), hT.a(kc, c0, c0 + cw)),
                                   rv.k(kc, None, j * 128, j * 128 + 128) + hT.k(kc, None, c0, c0 + cw), pk(b), start=(kc == 0), stop=(kc == 15))
                        sl = SL_[ctr[0] % 2]
                        ctr[0] += 1
                        ACT('activation', (sl.a(0, 0, cw), pss[bg][:, 0:cw], AF.Silu), pk(bg), sl.k(0, None, 0, cw))
                        hi = 2 * s + j
                        DVE('tensor_tensor', (HID.a(hi, c0, c0 + cw), sl.a(0, 0, cw), pss[bu][:, 0:cw], ALU.mult),
                            pk(bu) + sl.k(0, None, 0, cw), HID.k(hi, None, c0, c0 + cw))
            for i in range(8):
                Wd = ffn_w_down[l]
                r1 = load_slab(Wd[ft0 * 128:(ft0 + 16) * 128, :], 16, i * 256, 256)
                r2 = load_slab(Wd[(ft0 + 16) * 128:(ft0 + 22) * 128, :], 6, i * 256, 256)
                for j in range(2):
                    for (c0, cw) in colchunks(run):
                        b = nbank()
                        for kc in range(22):
                            rv, kk = (r1, kc) if kc < 16 else (r2, kc - 16)
                            PE('matmul', (pss[b][:, 0:cw], rv.a(kk, j * 128, j * 128 + 128), HID.a(kc, c0, c0 + cw)),
                               rv.k(kk, None, j * 128, j * 128 + 128) + HID.k(kc, None, c0, c0 + cw), pk(b), start=(kc == 0), stop=(kc == 21))
                        epi_resid(i * 2 + j, 128, c0, cw, b)

    def even_mixer(run, e):
        nt = 4 + (1 if run == 0 else 0)
        f = [scr_f0 + 1536]

        def fal(d1, d2):
            v = View(arf, f[0], d1, d2)
            f[0] += d1 * d2
            assert f[0] <= AF_N, f[0]
            return v
        bo = [scr_b0]

        def bal(d1, d2):
            v = View(arb, bo[0], d1, d2)
            bo[0] += d1 * d2
            assert bo[0] <= AB_N, bo[0]
            return v
        AT = fal(8, 544)
        AS = fal(8, NSMP)
        UT = fal(8, NCOL)
        VT = fal(2, 1088)
        T1 = View(arf, VT.off, 2, 544)
        T2 = View(arf, VT.off + 1088, 2, 544)
        MV = fal(1, 16)
        TMPA = fal(1, 256)
        PB = bal(8, NCOL)
        VN = bal(5, 1024)
        PW = bal(8, 256)
        WMT = bal(4, 128)
        DG = bal(4, 16)
        LNG = fal(1, 1024)
        LNB = fal(1, 1024)
        HS = fal(8, NSMP * 16) if run == 0 else None
        WST = bal(4, 128)
        DMA('sp', LNG.a(0), lnrep[e, 0], [], LNG.kall())
        DMA('sp', LNB.a(0), lnrep[e, 1], [], LNB.kall())
        DMA('pool', WST.a3(0, 4), wsT[e].rearrange("p (a b) -> p a b", b=128), [], WST.kall())
        DMA('pool', PW.a3(0, 8), poolw[e].rearrange("p (a b) -> p a b", b=256), [], PW.kall())
        for hh in range(4):
            DVE('tensor_tensor', (WMT.a(hh), WST.a(hh), CS(C_UT), ALU.mult), WST.k(hh) + CK, WMT.k(hh))
        if run == 0:
            DMA('sp', HS.a3(0, 8), spool[e].rearrange("p (a b) -> p a b", b=NSMP * 16), [], HS.kall())
            for hh in range(4):
                DVE('tensor_scalar', (DG.a(hh, 0, 16, 0, 16), CS(C_ID, 16, 16), P("w00_%d" % e, hh)[0:16, :], None, ALU.mult), CK + PK, DG.k(hh))
            DVE('memset', (AT.a3(0, 8, 0, 15), 0.0), [], AT.k(0, 8, 0, 15))
        else:
            DVE('tensor_copy', (AT.a3(0, 8, 0, 15), PH.a(e).rearrange("p (a b) -> p a b", b=15)), PH.k(e), AT.k(0, 8, 0, 15))

        def epi_in(ft, M, c0, cw, b):
            if ft < 8:
                if c0 == 0:
                    ACT('copy', (AT.a(ft, 15, 15 + cw), pss[b][:, 0:cw]), pk(b), AT.k(ft, None, 15, 15 + cw))
                else:
                    ACT('copy', (AS.a(ft), pss[b][:, 0:cw]), pk(b), AS.k(ft))
            else:
                ACT('activation', (UT.a(ft - 8, c0, c0 + cw), pss[b][:, 0:cw], AF.Gelu), pk(b), UT.k(ft - 8, None, c0, c0 + cw))
        stream_fm(run, ev_w_in[e], 16, [(i * 256, 256) for i in range(8)], hT, epi_in)
        rvs = [load_slab(ev_w_in[e], 16, 2048 + s4 * 256, 256) for s4 in range(4)]
        for tt in range(nt):
            c0, M = (tt * 128, 128) if tt < 4 else (T, NSMP)
            vt = tt % 2
            for s4 in range(4):
                rv = rvs[s4]
                b = nbank()
                for kc in range(16):
                    PE('matmul', (pss[b][0:M, 0:256], hT.a(kc, c0, c0 + M), rv.a(kc, 0, 256)),
                       rv.k(kc) + hT.k(kc, None, c0, c0 + M), pk(b), start=(kc == 0), stop=(kc == 15))
                ACT('activation', (VT.a(vt, s4 * 256, s4 * 256 + 256, 0, M), pss[b][0:M, 0:256], AF.Gelu), pk(b), VT.k(vt, None, s4 * 256, s4 * 256 + 256))
            for hf in range(2):
                DVE('bn_stats', (MV.a(0, hf * 6, hf * 6 + 6, 0, M), VT.a(vt, hf * 512, hf * 512 + 512, 0, M)), VT.k(vt), MV.kall())
            DVE('bn_aggr', (MV.a(0, 12, 14, 0, M), MV.a(0, 0, 12, 0, M)), MV.kall(), MV.kall())
            ACT('activation', (MV.a(0, 14, 15, 0, M), MV.a(0, 13, 14, 0, M), AF.Sqrt), MV.kall() + PK, MV.kall(), bias=P("eps")[0:M, :], scale=1.0)
            DVE('reciprocal', (MV.a(0, 15, 16, 0, M), MV.a(0, 14, 15, 0, M)), MV.kall(), MV.kall())
            DVE('tensor_scalar', (VT.a(vt, 0, 1024, 0, M), VT.a(vt, 0, 1024, 0, M), MV.a(0, 12, 13, 0, M), MV.a(0, 15, 16, 0, M), ALU.subtract, ALU.mult),
                VT.k(vt) + MV.kall(), VT.k(vt))
            DVE('tensor_tensor', (VT.a(vt, 0, 1024, 0, M), VT.a(vt, 0, 1024, 0, M), LNG.a(0, 0, 1024, 0, M), ALU.mult), VT.k(vt) + LNG.kall(), VT.k(vt))
            DVE('tensor_tensor', (VT.a(vt, 0, 1024, 0, M), VT.a(vt, 0, 1024, 0, M), LNB.a(0, 0, 1024, 0, M), ALU.add), VT.k(vt) + LNB.kall(), VT.k(vt))
            ACT('copy', (VN.a(tt, 0, 1024, 0, M), VT.a(vt, 0, 1024, 0, M)), VT.k(vt), VN.k(tt))
            if tt == 4:
                DMA('sp', o_v_s[e], VT.a(vt, 0, 1024, 0, NSMP), VT.k(vt), [])
        L = 15 + T
        if run == 0:
            hs4a = HS.a3(0, 8).rearrange("p a (s r) -> p a s r", r=16)
            ACT('copy', (hs4a[:, :, :, 15:16], AS.a3(0, 8).rearrange("p a (s r) -> p a s r", r=1)), AS.kall(), HS.kall())
        for g, w in enumerate((2, 4, 8, 16)):
            c0g = 2 * g
            src = AT
            si = c0g
            bufs = [T1, T2]
            sh = 1
            lvl = 0
            while sh < w:
                dst = bufs[lvl % 2]
                lo = 2 * sh - 1
                DVE('tensor_tensor', (dst.a3(0, 2, lo, L), src.a3(si, si + 2, lo, L), src.a3(si, si + 2, lo - sh, L - sh), ALU.add),
                    src.k(si, si + 2), dst.kall())
                src, si = dst, 0
                sh *= 2
                lvl += 1
            DVE('scalar_tensor_tensor', (PB.a3(c0g, c0g + 2, 0, T), src.a3(0, 2, 15, L), 1.0 / w, AT.a3(c0g, c0g + 2, 15, L), ALU.mult, ALU.subtract),
                src.kall() + AT.k(c0g, c0g + 2), PB.k(c0g, c0g + 2, 0, T))
            if run == 0:
                iv = CST.a(0, C_INV + g * 32, C_INV + g * 32 + 32).rearrange("p (a b) -> p a b", b=16)
                tm = TMPA.a(0, 0, 32).rearrange("p (a b) -> p a b", b=16)
                DVE('tensor_tensor', (tm, src.a3(0, 2, 15, 31), iv, ALU.mult), src.kall() + CK, TMPA.kall())
                DVE('tensor_tensor', (PB.a3(c0g, c0g + 2, 0, 16), tm, AT.a3(c0g, c0g + 2, 15, 31), ALU.subtract),
                    TMPA.kall() + AT.k(c0g, c0g + 2), PB.k(c0g, c0g + 2, 0, 16))
                hs4 = HS.a3(c0g, c0g + 2).rearrange("p a (s r) -> p a s r", r=16)[:, :, :, 16 - w:16]
                tm2 = TMPA.a(0, 64, 96).rearrange("p (a b) -> p a b", b=16)
                DVE('tensor_reduce', (tm2, hs4, AX.X, ALU.add), HS.k(c0g, c0g + 2), TMPA.kall())
                DVE('scalar_tensor_tensor', (PB.a3(c0g, c0g + 2, T, NCOL), tm2, 1.0 / w, AS.a3(c0g, c0g + 2), ALU.mult, ALU.subtract),
                    TMPA.kall() + AS.k(c0g, c0g + 2), PB.k(c0g, c0g + 2, T, NCOL))
        DVE('tensor_copy', (PH.a(e).rearrange("p (a b) -> p a b", b=15), AT.a3(0, 8, T, T + 15)), AT.kall(), PH.k(e))
        if run == NRUN - 1:
            DMA('sp', o_pool_p[e], PH.a(e), PH.k(e), [])
        if run == 0:
            DMA('sp', o_pool_s[e], HS.a3(0, 8).rearrange("p a b -> p (a b)"), HS.kall(), [])
        MO = hT
        for g in range(4):
            for dt in range(2):
                for (c0, cw) in colchunks(run):
                    b = nbank()
                    for k2 in range(2):
                        PE('matmul', (pss[b][:, 0:cw], PW.a(g * 2 + k2, dt * 128, dt * 128 + 128), PB.a(2 * g + k2, c0, c0 + cw)),
                           PW.k(g * 2 + k2) + PB.k(2 * g + k2, None, c0, c0 + cw), pk(b), start=(k2 == 0), stop=(k2 == 1))
                    ch = 2 * g + dt
                    DVE('tensor_scalar', (MO.a(ch, c0, c0 + cw), pss[b][:, 0:cw], P("pscale%d" % e, ch), None, ALU.mult), pk(b) + PK, MO.k(ch, None, c0, c0 + cw))
        for tt in range(nt):
            for hp in range(2):
                b = nbank()
                for q in range(4):
                    ch = hp * 4 + q
                    hh = ch // 2
                    if tt < 4:
                        PE('matmul', (pss[b][:, q * 128:q * 128 + 128], VN.a(tt, ch * 128, ch * 128 + 128), WMT.a(hh)),
                           VN.k(tt, None, ch * 128, ch * 128 + 128) + WMT.k(hh), pk(b), start=True, stop=True)
                    else:
                        PE('matmul', (pss[b][:, q * 128:q * 128 + 16], VN.a(4, ch * 128, ch * 128 + 128, 0, 16), DG.a(hh, 0, 16, 0, 16)),
                           VN.k(4, None, ch * 128, ch * 128 + 128) + DG.k(hh), pk(b), start=True, stop=True)
                for h2 in range(2):
                    hh = hp * 2 + h2
                    ch0 = hh * 2
                    if tt < 4:
                        tm = TMPA.a(0, 0, 256).rearrange("p (a b) -> p a b", b=128)
                        bsr = P("bsrep%d" % e, hh * 128, 128)
                        for c2 in range(2):
                            q0 = h2 * 256 + c2 * 128
                            DVE('tensor_tensor', (TMPA.a(0, c2 * 128, c2 * 128 + 128), pss[b][:, q0:q0 + 128], bsr, ALU.add), pk(b) + PK, TMPA.kall())
                        DVE('tensor_tensor', (MO.a3(8 + ch0, 8 + ch0 + 2, tt * 128, tt * 128 + 128), tm, UT.a3(ch0, ch0 + 2, tt * 128, tt * 128 + 128), ALU.mult),
                            TMPA.kall() + UT.k(ch0, ch0 + 2, tt * 128, tt * 128 + 128), MO.k(8 + ch0, 8 + ch0 + 2, tt * 128, tt * 128 + 128))
                    else:
                        tm = TMPA.a(0, 0, 32).rearrange("p (a b) -> p a b", b=16)
                        pv = pss[b][:, h2 * 256:h2 * 256 + 256].rearrange("p (a b) -> p a b", b=128)[:, :, 0:16]
                        DVE('tensor_scalar', (tm, pv, P("bs0_%d" % e, hh), None, ALU.add), pk(b) + PK, TMPA.kall())
                        DVE('tensor_tensor', (MO.a3(8 + ch0, 8 + ch0 + 2, T, NCOL), tm, UT.a3(ch0, ch0 + 2, T, NCOL), ALU.mult),
                            TMPA.kall() + UT.k(ch0, ch0 + 2, T, NCOL), MO.k(8 + ch0, 8 + ch0 + 2, T, NCOL))
        stream_fm(run, ev_w_out[e], 16, [(i * 256, 256) for i in range(8)], MO, epi_resid)

    def odd_mixer(run, o):
        nt = 4 + (1 if run == 0 else 0)
        f = [scr_f0 + 1536]

        def fal(d1, d2):
            v = View(arf, f[0], d1, d2)
            f[0] += d1 * d2
            assert f[0] <= AF_N, f[0]
            return v
        ABT = fal(5, 16)
        GG = fal(5, 8)
        BB = fal(5, 8)
        SM = fal(1, 64)
        dn0 = f[0]
        OTs = View(arf, dn0 + 8192, 8, NCOL)
        GT = fal(8, 560)
        GS = fal(8, NSMP)
        YT = fal(8, NCOL)
        SIG = fal(2, 512)
        ct0 = f[0] - 1024
        MEAN = fal(1, 512)
        MSQ = fal(1, 512)
        RSTD = fal(1, 512)
        HS2 = fal(8, NSMP * 31) if run == 0 else None
        QKV = View(arb, scr_b0, 24, NCOL)
        ZG = View(arb, scr_b0 + 24 * NCOL, 8, NCOL)
        MO = hT
        CW = lambda kc, j0=0, n=31: P("cw%d" % o, kc * 31 + j0, n)
        rv = load_slab(od_w_in[o], 16, 6144, 16)
        for tt in range(nt):
            c0, M = (tt * 128, 128) if tt < 4 else (T, NSMP)
            b = nbank()
            for kc in range(16):
                PE('matmul', (pss[b][0:M, 0:16], hT.a(kc, c0, c0 + M), rv.a(kc, 0, 16)), rv.k(kc, None, 0, 16) + hT.k(kc, None, c0, c0 + M), pk(b), start=(kc == 0), stop=(kc == 15))
            ACT('copy', (ABT.a(tt, 0, 16, 0, M), pss[b][0:M, 0:16]), pk(b), ABT.k(tt))
        NEA = SM.a(0, 0, 8)
        ACT('activation', (NEA, P("alog%d" % o, 0, 8), AF.Exp), PK, SM.kall())
        DVE('tensor_scalar', (NEA, NEA, -1.0, None, ALU.mult), SM.kall(), SM.kall())
        for tt in range(nt):
            M = 128 if tt < 4 else NSMP
            x_ = SM.a(0, 8, 16, 0, M)
            ax = SM.a(0, 16, 24, 0, M)
            lg = SM.a(0, 24, 32, 0, M)
            DVE('tensor_tensor', (x_, ABT.a(tt, 0, 8, 0, M), P("dtb%d" % o, 0, 8)[0:M, :], ALU.add), ABT.k(tt) + PK, SM.kall())
            ACT('activation', (ax, x_, AF.Abs), SM.kall(), SM.kall())
            ACT('activation', (ax, ax, AF.Exp), SM.kall(), SM.kall(), scale=-1.0)
            ACT('activation', (lg, ax, AF.Ln), SM.kall() + CK, SM.kall(), bias=CS(C_ONE, 1, M), scale=1.0)
            DVE('tensor_scalar', (x_, x_, 0.0, None, ALU.max), SM.kall(), SM.kall())
            DVE('tensor_tensor', (x_, x_, lg, ALU.add), SM.kall(), SM.kall())
            DVE('tensor_tensor', (GG.a(tt, 0, 8, 0, M), x_, SM.a(0, 0, 8, 0, M), ALU.mult), SM.kall(), GG.k(tt))
            ACT('activation', (BB.a(tt, 0, 8, 0, M), ABT.a(tt, 8, 16, 0, M), AF.Sigmoid), ABT.k(tt), BB.k(tt))
        if run == 0:
            DVE('memset', (GT.a3(0, 8, 0, 30), 0.0), [], GT.k(0, 8, 0, 30))
            DMA('sp', HS2.a3(0, 8), sconv[o].rearrange("p (a b) -> p a b", b=NSMP * 31), [], HS2.kall())
        else:
            DVE('tensor_copy', (GT.a3(0, 8, 0, 30), CH.a(o).rearrange("p (a b) -> p a b", b=30)), CH.k(o), GT.k(0, 8, 0, 30))
        sctr = [0]

        def epi_c(ft, M, c0, cw, b):
            if ft < 8:
                if c0 == 0:
                    ACT('copy', (GT.a(ft, 30, 30 + cw), pss[b][:, 0:cw]), pk(b), GT.k(ft, None, 30, 30 + cw))
                else:
                    ACT('copy', (GS.a(ft), pss[b][:, 0:cw]), pk(b), GS.k(ft))
            else:
                sg = SIG.a(sctr[0] % 2, 0, cw)
                sgk = SIG.k(sctr[0] % 2)
                sctr[0] += 1
                ACT('activation', (sg, pss[b][:, 0:cw], AF.Sigmoid), pk(b), sgk)
                if c0 == 0:
                    DVE('tensor_tensor', (GT.a(ft - 8, 30, 30 + cw), GT.a(ft - 8, 30, 30 + cw), sg, ALU.mult), sgk + GT.k(ft - 8, None, 30, 30 + cw), GT.k(ft - 8, None, 30, 30 + cw))
                else:
                    DVE('tensor_tensor', (GS.a(ft - 8), GS.a(ft - 8), sg, ALU.mult), sgk + GS.k(ft - 8), GS.k(ft - 8))
        stream_fm(run, od_w_in[o], 16, [(i * 256, 256) for i in range(8)], hT, epi_c)
        DVE('tensor_copy', (CH.a(o).rearrange("p (a b) -> p a b", b=30), GT.a3(0, 8, T, T + 30)), GT.kall(), CH.k(o))
        if run == NRUN - 1:
            DMA('sp', o_conv_p[o], CH.a(o), CH.k(o), [])
        def conv_taps():
            for kc in range(8):
                DVE('tensor_scalar', (YT.a(kc, 0, T), GT.a(kc, 0, T), CW(kc, 0, 1), P("cdb%d" % o, kc), ALU.mult, ALU.add), GT.k(kc) + PK, YT.k(kc, None, 0, T))
            yield
            for j in range(1, 31):
                for kc in range(8):
                    DVE('scalar_tensor_tensor', (YT.a(kc, 0, T), GT.a(kc, j, j + T), CW(kc, j, 1), YT.a(kc, 0, T), ALU.mult, ALU.add), GT.k(kc) + PK + YT.k(kc, None, 0, T), YT.k(kc, None, 0, T))
                yield
        conv_gen = conv_taps()

        def conv_advance(nsteps=2):
            for _ in range(nsteps):
                try:
                    next(conv_gen)
                except StopIteration:
                    return
        if run == 0:
            h4 = HS2.a3(0, 8).rearrange("p a (s r) -> p a s r", r=31)
            ACT('copy', (h4[:, :, :, 30:31], GS.a3(0, 8).rearrange("p a (s r) -> p a s r", r=1)), GS.kall(), HS2.kall())
            DMA('sp', o_conv_s[o], HS2.a3(0, 8).rearrange("p a b -> p (a b)"), HS2.kall(), [])
            cw3 = P("cw%d" % o, 0, 8 * 31).rearrange("p (a b) -> p a b", b=31)
            for s in range(NSMP):
                DVE('tensor_tensor', (h4[:, :, s, :], h4[:, :, s, :], cw3, ALU.mult), HS2.kall() + PK, HS2.kall())
            DVE('tensor_reduce', (YT.a3(0, 8, T, NCOL), h4, AX.X, ALU.add), HS2.kall(), YT.k(0, 8, T, NCOL))
            for kc in range(8):
                DVE('tensor_scalar', (YT.a(kc, T, NCOL), YT.a(kc, T, NCOL), P("cdb%d" % o, kc), None, ALU.add), YT.k(kc, None, T, NCOL) + PK, YT.k(kc, None, T, NCOL))
        f[0] = ct0
        assert ct0 == dn0 + 8832
        PRE = fal(2, 544)
        CV = fal(2, 512)
        f[0] = dn0 + 12416
        SQ2 = fal(2, 512)
        RS2 = fal(1, 512)
        QS = fal(24, NSMP) if run == 0 else None
        HS3 = fal(24, NSMP * 4) if run == 0 else None
        if run == 0:
            DMA('sp', HS3.a3(0, 24), sdnc[o].rearrange("p (a b) -> p a b", b=NSMP * 4), [], HS3.kall())
            DVE('memset', (DH.a(o), 0.0), [], DH.k(o))
        DWp = lambda ft, j: P("dw%d" % o, ft * 4 + j)
        pctr = [0]

        def epi_q(ft, M, c0, cw, b):
            fq = ft - 16
            if fq >= 24:
                ACT('activation', (ZG.a(fq - 24, c0, c0 + cw), pss[b][:, 0:cw], AF.Silu), pk(b), ZG.k(fq - 24, None, c0, c0 + cw))
                return
            i2 = pctr[0] % 2
            pctr[0] += 1
            cv, cvk = CV.a(i2, 0, cw), CV.k(i2)
            if c0 == 0:
                pre, prk = PRE, PRE.k(i2)
                dh = DH.a(o, fq * 3, fq * 3 + 3)
                DVE('tensor_copy', (PRE.a(i2, 0, 3), dh), DH.k(o), prk)
                ACT('copy', (PRE.a(i2, 3, 3 + cw), pss[b][:, 0:cw]), pk(b), prk)
                DVE('tensor_copy', (dh, PRE.a(i2, cw, cw + 3)), prk, DH.k(o))
                DVE('tensor_scalar', (cv, PRE.a(i2, 0, cw), DWp(fq, 0), None, ALU.mult), prk + PK, cvk)
                for j in range(1, 4):
                    DVE('scalar_tensor_tensor', (cv, PRE.a(i2, j, j + cw), DWp(fq, j), cv, ALU.mult, ALU.add), prk + PK + cvk, cvk)
            else:
                h4 = HS3.a(fq).rearrange("p (s r) -> p s r", r=4)
                ACT('copy', (h4[:, :, 3:4], pss[b][:, 0:cw].rearrange("p (s r) -> p s r", r=1)), pk(b), HS3.k(fq))
                cv3 = cv.rearrange("p (s r) -> p s r", r=1)
                DVE('tensor_scalar', (cv3, h4[:, :, 0:1], DWp(fq, 0), None, ALU.mult), HS3.k(fq) + PK, cvk)
                for j in range(1, 4):
                    DVE('scalar_tensor_tensor', (cv3, h4[:, :, j:j + 1], DWp(fq, j), cv3, ALU.mult, ALU.add), HS3.k(fq) + PK + cvk, cvk)
            ACT('activation', (cv, cv, AF.Silu), cvk, cvk)
            if fq < 16:
                s2, s2k = SQ2.a(i2, 0, cw), SQ2.k(i2)
                ACT('activation', (s2, cv, AF.Square), cvk, s2k)
                bn = nbank()
                PE('matmul', (pss[bn][:, 0:cw], CS(C_ONE), s2), s2k + CK, pk(bn), start=True, stop=True)
                r2 = RS2.a(0, 0, cw)
                ACT('activation', (r2, pss[bn][:, 0:cw], AF.Sqrt), pk(bn) + PK, RS2.kall(), bias=P("eps"), scale=1.0)
                DVE('reciprocal', (r2, r2), RS2.kall(), RS2.kall())
                sc = 128.0 ** -0.5 if fq < 8 else 1.0
                DVE('scalar_tensor_tensor', (QKV.a(fq, c0, c0 + cw), cv, sc, r2, ALU.mult, ALU.mult), cvk + RS2.kall(), QKV.k(fq, None, c0, c0 + cw))
                if c0 != 0:
                    DVE('scalar_tensor_tensor', (QS.a(fq), cv, sc, r2, ALU.mult, ALU.mult), cvk + RS2.kall(), QS.k(fq))
            else:
                ACT('copy', (QKV.a(fq, c0, c0 + cw), cv), cvk, QKV.k(fq, None, c0, c0 + cw))
                if c0 != 0:
                    ACT('copy', (QS.a(fq), cv), cvk, QS.k(fq))
        stream_fm(run, od_w_in[o], 16, [(2048 + i * 256, 256) for i in range(16)], hT, epi_q, after_slab=conv_advance)
        conv_advance(100)
        for (c0, cw) in colchunks(run):
            b1 = nbank()
            b2 = nbank()
            for kc in range(8):
                sq = SIG.a(kc % 2, 0, cw)
                PE('matmul', (pss[b1][:, 0:cw], CS(C_ONE), YT.a(kc, c0, c0 + cw)), YT.k(kc, None, c0, c0 + cw) + CK, pk(b1), start=(kc == 0), stop=(kc == 7))
                ACT('activation', (sq, YT.a(kc, c0, c0 + cw), AF.Square), YT.k(kc, None, c0, c0 + cw), SIG.k(kc % 2))
                PE('matmul', (pss[b2][:, 0:cw], CS(C_ONE), sq), SIG.k(kc % 2) + CK, pk(b2), start=(kc == 0), stop=(kc == 7))
            mn, ms, rs = MEAN.a(0, 0, cw), MSQ.a(0, 0, cw), RSTD.a(0, 0, cw)
            ACT('activation', (mn, pss[b1][:, 0:cw], AF.Copy), pk(b1), MEAN.kall(), scale=1.0 / 1024)
            DVE('tensor_tensor', (ms, mn, mn, ALU.mult), MEAN.kall(), MSQ.kall())
            DVE('scalar_tensor_tensor', (ms, pss[b2][:, 0:cw], 1.0 / 1024, ms, ALU.mult, ALU.subtract), pk(b2) + MSQ.kall(), MSQ.kall())
            ACT('activation', (rs, ms, AF.Sqrt), MSQ.kall() + PK, RSTD.kall(), bias=P("eps"), scale=1.0)
            DVE('reciprocal', (rs, rs), RSTD.kall(), RSTD.kall())
            for kc in range(8):
                yk = YT.k(kc, None, c0, c0 + cw)
                DVE('tensor_tensor', (YT.a(kc, c0, c0 + cw), YT.a(kc, c0, c0 + cw), mn, ALU.subtract), yk + MEAN.kall(), yk)
                DVE('tensor_tensor', (YT.a(kc, c0, c0 + cw), YT.a(kc, c0, c0 + cw), rs, ALU.mult), yk + RSTD.kall(), yk)
        if run == NRUN - 1:
            DMA('sp', o_dnc_p[o], DH.a(o), DH.k(o), [])
        if run == 0:
            DMA('sp', o_dnc_s[o], HS3.a3(0, 24).rearrange("p a b -> p (a b)"), HS3.kall(), [])
            DVE('memset', (SST.a(o), 0.0), [], SST.k(o))
        for (c0, cw) in colchunks(run):
            for kc in range(8):
                ACT('activation', (MO.a(kc, c0, c0 + cw), YT.a(kc, c0, c0 + cw), AF.Silu), YT.k(kc, None, c0, c0 + cw) + PK, MO.k(kc, None, c0, c0 + cw),
                    bias=P("clb%d" % o, kc), scale=P("clg%d" % o, kc))
        f[0] = dn0
        GC, GL, EG, KDF, EGL, NBt, BEG = [View(arf, SM.off + 32 + 0 * 8, 1, 8)] * 7
        f[0] = dn0
        smalls = [fal(1, 8) for _ in range(7)]
        GC, GL, EG, KDF, EGL, NBt, BEG = smalls
        f[0] = dn0 + 64
        tsets = [[fal(1, 128) for _ in range(15)] for _ in range(4)]
        assert f[0] <= dn0 + 8192, f[0]
        IDF = CS(C_ID)

        def dn_head(n, h, TS_, bk_):
            cs, ce = n * 128, n * 128 + 128
            tA, tB, tC, tD, M1, N1, QKM, MA, MB, NA, NBb, QA, QB, U_, WT = TS_
            VNW = MA
            ps_ = lambda s_: pss[bk_][:, s_ * 128:s_ * 128 + 128]
            qn, qnk = QKV.a(h, cs, ce), QKV.k(h, None, cs, ce)
            kn, knk = QKV.a(8 + h, cs, ce), QKV.k(8 + h, None, cs, ce)
            vv, vvk = QKV.a(16 + h, cs, ce), QKV.k(16 + h, None, cs, ce)
            Sv = SST.a(o, h * 128, h * 128 + 128)
            Sk = SST.k(o, None, h * 128, h * 128 + 128)
            DVE('tensor_scalar', (tA.a(0), CS(C_ONE), GG.a(n, h, h + 1), None, ALU.mult), GG.k(n) + CK, tA.kall())
            yield
            PE('matmul', (ps_(0), tA.a(0), CS(C_UT)), tA.kall() + CK, pks(bk_, 0), start=True, stop=True)
            PE('matmul', (ps_(1), kn, kn), knk, pks(bk_, 1), start=True, stop=True)
            PE('matmul', (ps_(2), kn, qn), knk + qnk, pks(bk_, 2), start=True, stop=True)
            gcc = GC.a(0, h, h + 1)
            DVE('tensor_scalar', (tB.a(0), ps_(0), gcc, 0.0, ALU.subtract, ALU.max), pks(bk_, 0) + GC.kall(), tB.kall())
            ACT('activation', (tB.a(0), tB.a(0), AF.Exp), tB.kall(), tB.kall(), scale=-1.0)
            DVE('tensor_scalar', (tC.a(0), ps_(0), gcc, 0.0, ALU.subtract, ALU.min), pks(bk_, 0) + GC.kall(), tC.kall())
            ACT('activation', (tC.a(0), tC.a(0), AF.Exp), tC.kall(), tC.kall())
            ACT('activation', (tD.a(0), ps_(0), AF.Exp), pks(bk_, 0), tD.kall())
            DVE('tensor_tensor', (tA.a(0), ps_(1), tB.a(0), ALU.mult), pks(bk_, 1) + tB.kall(), tA.kall())
            DVE('scalar_tensor_tensor', (M1.a(0), tA.a(0), NBt.a(0, h, h + 1), CS(C_SL), ALU.mult, ALU.mult), tA.kall() + NBt.kall() + CK, M1.kall())
            yield
            PE('transpose', (ps_(3), M1.a(0), IDF), M1.kall() + CK, pks(bk_, 3))
            ACT('copy', (N1.a(0), ps_(3)), pks(bk_, 3), N1.kall())
            DVE('tensor_tensor', (tC.a(0), ps_(2), tC.a(0), ALU.mult), pks(bk_, 2) + tC.kall(), tC.kall())
            DVE('tensor_tensor', (QKM.a(0), tC.a(0), CS(C_UT), ALU.mult), tC.kall() + CK, QKM.kall())
            DVE('tensor_tensor', (QA.a(0), N1.a(0), IDF, ALU.add), N1.kall() + CK, QA.kall())
            PE('transpose', (psb[:, 0:128], kn, IDB.a(0)), knk + IDB.kall(), [('psb', 0)])
            PE('transpose', (psb[:, 128:256], vv, IDB.a(0)), vvk + IDB.kall(), [('psb', 0)])
            DVE('tensor_scalar', (tA.a(0), psb[:, 128:256], BB.a(n, h, h + 1), None, ALU.mult), [('psb', 0)] + BB.k(n), tA.kall())
            DVE('tensor_scalar', (tB.a(0), psb[:, 0:128], BEG.a(0, h, h + 1), None, ALU.mult), [('psb', 0)] + BEG.kall(), tB.kall())
            DVE('tensor_scalar', (tC.a(0), psb[:, 0:128], KDF.a(0, h, h + 1), None, ALU.mult), [('psb', 0)] + KDF.kall(), tC.kall())
            DVE('tensor_tensor', (tD.a(0), qn, tD.a(0), ALU.mult), qnk + tD.kall(), tD.kall())
            yield
            Mc, Nc, Qc = M1, N1, QA
            Mo, No, Qo = [MA, MB], [NA, NBb], [QB, QA]
            for lv in range(6):
                Mn, Nn, Qn = Mo[lv % 2], No[lv % 2], Qo[lv % 2]
                PE('matmul', (ps_(0), Nc.a(0), Mc.a(0)), Nc.kall() + Mc.kall(), pks(bk_, 0), start=True, stop=True)
                if lv < 5:
                    PE('matmul', (ps_(1), Mc.a(0), Nc.a(0)), Nc.kall() + Mc.kall(), pks(bk_, 1), start=True, stop=True)
                if bk_ % 2 == 0:
                    ACT('copy', (Mn.a(0), ps_(0)), pks(bk_, 0), Mn.kall())
                    if lv < 5:
                        ACT('copy', (Nn.a(0), ps_(1)), pks(bk_, 1), Nn.kall())
                else:
                    DVE('tensor_copy', (Mn.a(0), ps_(0)), pks(bk_, 0), Mn.kall())
                    if lv < 5:
                        DVE('tensor_copy', (Nn.a(0), ps_(1)), pks(bk_, 1), Nn.kall())
                yield
                PE('matmul', (ps_(2), Mn.a(0), Qc.a(0)), Mn.kall() + Qc.kall(), pks(bk_, 2), start=True, stop=True)
                DVE('tensor_tensor', (Qn.a(0), Qc.a(0), ps_(2), ALU.add), Qc.kall() + pks(bk_, 2), Qn.kall())
                Mc, Nc, Qc = Mn, Nn, Qn
                yield
            PE('matmul', (ps_(0), Qc.a(0), tA.a(0)), Qc.kall() + tA.kall(), pks(bk_, 0), start=True, stop=True)
            PE('matmul', (ps_(1), tB.a(0), Qc.a(0)), Qc.kall() + tB.kall(), pks(bk_, 1), start=True, stop=True)
            ACT('copy', (U_.a(0), ps_(0)), pks(bk_, 0), U_.kall())
            ACT('copy', (WT.a(0), ps_(1)), pks(bk_, 1), WT.kall())
            yield
            PE('matmul', (ps_(2), WT.a(0), Sv), WT.kall() + Sk, pks(bk_, 2), start=True, stop=True)
            DVE('tensor_tensor', (VNW.a(0), U_.a(0), ps_(2), ALU.subtract), U_.kall() + pks(bk_, 2), VNW.kall())
            PE('matmul', (ps_(3), Sv, tD.a(0)), Sk + tD.kall(), pks(bk_, 3), start=True, stop=False)
            yield
            PE('matmul', (ps_(3), VNW.a(0), QKM.a(0)), VNW.kall() + QKM.kall(), pks(bk_, 3), start=False, stop=True)
            ACT('copy', (OTs.a(h, cs, ce), ps_(3)), pks(bk_, 3), OTs.k(h, None, cs, ce))
            PE('matmul', (ps_(0), tC.a(0), VNW.a(0)), tC.kall() + VNW.kall(), pks(bk_, 0), start=True, stop=True)
            DVE('scalar_tensor_tensor', (Sv, Sv, EGL.a(0, h, h + 1), ps_(0), ALU.mult, ALU.add), Sk + EGL.kall() + pks(bk_, 0), Sk)
            yield

        for n in range(4):
            b = 4 + (n % 3)
            PE('matmul', (pss[b][:, 0:8], CS(C_UT), GG.a(n)), GG.k(n) + CK, pks(b, 0), start=True, stop=True)
            PE('matmul', (pss[b][:, 8:16], CS(C_ONE), GG.a(n)), GG.k(n) + CK, pks(b, 0), start=True, stop=True)
            ACT('copy', (GC.a(0), pss[b][:, 0:8]), pks(b, 0), GC.kall())
            ACT('copy', (GL.a(0), pss[b][:, 8:16]), pks(b, 0), GL.kall())
            ACT('activation', (EG.a(0), GC.a(0), AF.Exp), GC.kall(), EG.kall())
            DVE('tensor_tensor', (KDF.a(0), GL.a(0), GC.a(0), ALU.subtract), GL.kall() + GC.kall(), KDF.kall())
            ACT('activation', (KDF.a(0), KDF.a(0), AF.Exp), KDF.kall(), KDF.kall())
            ACT('activation', (EGL.a(0), GL.a(0), AF.Exp), GL.kall(), EGL.kall())
            DVE('tensor_scalar', (NBt.a(0), BB.a(n), -1.0, None, ALU.mult), BB.k(n), NBt.kall())
            DVE('tensor_tensor', (BEG.a(0), BB.a(n), EG.a(0), ALU.mult), BB.k(n) + EG.kall(), BEG.kall())
            for hg in range(2):
                gens = [dn_head(n, hg * 4 + i, tsets[i], i) for i in range(4)]
                if os.environ.get('K_SEQ'):
                    for g_ in gens:
                        for _ in g_:
                            pass
                    gens = []
                alive = list(gens)
                while alive:
                    nxt = []
                    for g_ in alive:
                        try:
                            next(g_)
                            nxt.append(g_)
                        except StopIteration:
                            pass
                    alive = nxt
        bank_ctr[0] = 0
        if run == NRUN - 1:
            DMA('sp', o_S_p[o], SST.a(o), SST.k(o), [])
        if run == 0:
            f[0] = dn0 + 64
            RR = fal(1, 128)
            EGB = fal(1, 128)
            BEB = fal(1, 128)
            SS2 = [fal(8, 128), fal(8, 128)]
            TS = fal(1, 32)
            VREP = fal(1, 128)
            T3 = fal(1, 128)
            assert f[0] <= dn0 + 8192, f[0]
            ACT('activation', (EG.a(0, 0, 8, 0, NSMP), GG.a(4, 0, 8, 0, NSMP), AF.Exp), GG.k(4), EG.kall())
            for (srcv, srck, dst) in ((EG.a(0, 0, 8, 0, NSMP), EG.kall(), EGB), (BB.a(4, 0, 8, 0, NSMP), BB.k(4), BEB)):
                r3 = RR.a(0, 0, 128, 0, NSMP).rearrange("p (s h) -> p s h", h=8)
                for h in range(8):
                    DVE('tensor_scalar', (r3[:, :, h:h + 1], CS(C_ID, 16, 16).rearrange("p (s h) -> p s h", h=1), srcv[:, h:h + 1], None, ALU.mult), srck + CK, RR.kall())
                bq = nbank()
                PE('matmul', (pss[bq][:, 0:128], CS(C_ONE, 128, NSMP), RR.a(0, 0, 128, 0, NSMP)), RR.kall() + CK, pk(bq), start=True, stop=True)
                ACT('copy', (dst.a(0), pss[bq][:, 0:128]), pk(bq), dst.kall())
            for s in range(NSMP):
                SS = SS2[s % 2]
                DMA('sp', SS.a3(0, 8).rearrange("p a b -> p (a b)"), sS[o, s], [], SS.kall())
                bk = nbank()
                for h in range(8):
                    PE('matmul', (pss[bk][:, h:h + 1], SS.a(h), QS.a(8 + h, s, s + 1)), SS.k(h) + QS.k(8 + h), pk(bk), start=True, stop=True)
                t8 = TS.a(0, 0, 8)
                DVE('tensor_tensor', (t8, pss[bk][:, 0:8], EGB.a(0, s * 8, s * 8 + 8), ALU.mult), pk(bk) + EGB.kall(), TS.kall())
                vs8 = QS.a3(16, 24, s, s + 1).rearrange("p a b -> p (a b)")
                DVE('tensor_tensor', (t8, vs8, t8, ALU.subtract), QS.k(16, 24) + TS.kall(), TS.kall())
                DVE('tensor_tensor', (t8, t8, BEB.a(0, s * 8, s * 8 + 8), ALU.mult), BEB.kall() + TS.kall(), TS.kall())
                for h in range(8):
                    DVE('tensor_scalar', (VREP.a(0), CS(C_ONE), TS.a(0, h, h + 1), None, ALU.mult), TS.kall() + CK, VREP.kall())
                    bv = nbank()
                    PE('matmul', (pss[bv][:, 0:128], VREP.a(0), IDF), VREP.kall() + CK, pk(bv), start=True, stop=True)
                    DVE('tensor_scalar', (T3.a(0), SS.a(h), EGB.a(0, s * 8 + h, s * 8 + h + 1), None, ALU.mult), SS.k(h) + EGB.kall(), T3.kall())
                    DVE('scalar_tensor_tensor', (SS.a(h), pss[bv][:, 0:128], QS.a(8 + h, s, s + 1), T3.a(0), ALU.mult, ALU.add), pk(bv) + QS.k(8 + h) + T3.kall(), SS.k(h))
                bo2 = nbank()
                for h in range(8):
                    PE('matmul', (pss[bo2][:, h:h + 1], SS.a(h), QS.a(h, s, s + 1)), SS.k(h) + QS.k(h), pk(bo2), start=True, stop=True)
                ACT('copy', (OTs.a3(0, 8, T + s, T + s + 1).rearrange("p a b -> p (a b)"), pss[bo2][:, 0:8]), pk(bo2), OTs.k(0, 8, T + s, T + s + 1))
                DMA('sp', o_S_s[o, s], SS.a3(0, 8).rearrange("p a b -> p (a b)"), SS.kall(), [])
        for h in range(8):
            for (c0, cw) in colchunks(run):
                ok = OTs.k(h, None, c0, c0 + cw)
                s2 = SQ2.a(0, 0, cw)
                ACT('activation', (s2, OTs.a(h, c0, c0 + cw), AF.Square), ok, SQ2.k(0))
                bn = nbank()
                PE('matmul', (pss[bn][:, 0:cw], CS(C_ONE), s2), SQ2.k(0) + CK, pk(bn), start=True, stop=True)
                r2 = RS2.a(0, 0, cw)
                ACT('activation', (r2, pss[bn][:, 0:cw], AF.Sqrt), pk(bn) + PK, RS2.kall(), bias=P("eps"), scale=1.0 / 128)
                DVE('reciprocal', (r2, r2), RS2.kall(), RS2.kall())
                DVE('scalar_tensor_tensor', (OTs.a(h, c0, c0 + cw), OTs.a(h, c0, c0 + cw), P("ng%d" % o), r2, ALU.mult, ALU.mult), ok + PK + RS2.kall(), ok)
                DVE('tensor_tensor', (MO.a(8 + h, c0, c0 + cw), OTs.a(h, c0, c0 + cw), ZG.a(h, c0, c0 + cw), ALU.mult), ok + ZG.k(h, None, c0, c0 + cw), MO.k(8 + h, None, c0, c0 + cw))
        stream_fm(run, od_w_out[o], 16, [(i * 256, 256) for i in range(8)], MO, epi_resid)

    for run in range(NRUN):
        DMA('sp', xT.a3(0, 16, 0, T), xin[:, :, run * T:(run + 1) * T], [], xT.k(0, 16, 0, T))
        if run == 0:
            DMA('sp', xT.a3(0, 16, T, NCOL), xin[:, :, 2048:2048 + NSMP], [], xT.k(0, 16, T, NCOL))
        for l in range(DBG_LAYERS):
            rmsnorm(run, "nmix%d" % l)
            if l % 2 == 0:
                even_mixer(run, l // 2)
            else:
                odd_mixer(run, l // 2)
            rmsnorm(run, "nffn%d" % l)
            ffn(run, l)
        YO = View(arf, scr_f0 + 1536, 16, NCOL)
        rmsnorm(run, "nfin", out_bf16=False, outv=YO)
        DMA('sp', yout[:, :, run * T:(run + 1) * T], YO.a3(0, 16, 0, T), YO.k(0, 16, 0, T), [])
        if run == 0:
            DMA('sp', yout[:, :, 2048:2048 + NSMP], YO.a3(0, 16, T, NCOL), YO.k(0, 16, T, NCOL), [])
    S.emit(nc, es)
    es.close()
    return nc


def _fm(v, nch):
    sh = v.shape[:-1]
    a = v.reshape(sh + (nch, 128))
    a = np.moveaxis(a, -1, 0)
    return np.ascontiguousarray(a)


_NC_CACHE = {}


def kernel(**inp):
    f32 = np.float32
    g = lambda k: np.asarray(inp[k], dtype=f32)
    x_prompt, x_sample = g('x_prompt'), g('x_sample')
    if 'nc' not in _NC_CACHE:
        _NC_CACHE['nc'] = build_program()
    nc = _NC_CACHE['nc']
    L = PLAYOUT
    prm = np.zeros((128, 2048), f32)

    def put(name, arr):
        arr = np.asarray(arr, f32)
        if arr.ndim == 1:
            arr = np.broadcast_to(arr[None, :], (128, arr.shape[0]))
        prm[:, L[name]:L[name] + arr.shape[1]] = arr
    for l in range(4):
        put("nmix%d" % l, _fm(g('norm_mix')[l], 16))
        put("nffn%d" % l, _fm(g('norm_ffn')[l], 16))
    put("nfin", _fm(g('norm_final'), 16))
    for e in range(2):
        put("pscale%d" % e, _fm(g('pool_scale')[e], 8))
        put("bs0_%d" % e, g('sgu_b')[e][:, 0])
        put("w00_%d" % e, g('sgu_ws')[e][:, 0, 0])
        put("bsrep%d" % e, g('sgu_b')[e].reshape(-1))
    for o in range(2):
        put("cw%d" % o, _fm(g('cv_dw')[o], 8).transpose(0, 2, 1).reshape(128, -1))
        put("cdb%d" % o, _fm(g('cv_db')[o], 8))
        put("clg%d" % o, _fm(g('cv_ln_g')[o], 8))
        put("clb%d" % o, _fm(g('cv_ln_b')[o], 8))
        put("dw%d" % o, _fm(g('dn_conv_w')[o], 24).transpose(0, 2, 1).reshape(128, -1))
        put("ng%d" % o, g('dn_norm_g')[o][:, None])
        put("alog%d" % o, g('dn_a_log')[o])
        put("dtb%d" % o, g('dn_dt_bias')[o])
    put("eps", np.full((1,), EPS, f32))
    lnrep = np.stack([np.stack([np.broadcast_to(g('sgu_ln_g')[e][None], (128, 1024)),
                                np.broadcast_to(g('sgu_ln_b')[e][None], (128, 1024))]) for e in range(2)]).astype(f32)
    pw = g('pool_w')
    poolw = np.ascontiguousarray(pw.reshape(2, 4, 2, 128, 256).transpose(0, 3, 1, 2, 4).reshape(2, 128, -1))
    wsT = np.ascontiguousarray(g('sgu_ws').transpose(0, 3, 1, 2).reshape(2, 128, -1))
    idx = np.arange(128)
    ident = np.eye(128, dtype=f32)
    ut = (idx[:, None] <= idx[None, :]).astype(f32)
    sl = (idx[:, None] > idx[None, :]).astype(f32)
    ones = np.ones((128, 128), f32)
    inv = np.zeros((4, 2, 16), f32)
    for gi, w in enumerate((2, 4, 8, 16)):
        inv[gi, :, :] = 1.0 / np.minimum(w, np.arange(16) + 1)
    cst = np.concatenate([ident, ut, sl, ones, ones, ones, np.broadcast_to(inv.reshape(1, -1), (128, 128))], axis=1).astype(f32)
    cst = np.ascontiguousarray(cst)
    in_maps = []
    for c in range(8):
        b = c // 2
        s0 = c * NSMP
        xin = np.concatenate([_fm(x_prompt[b], 16), _fm(x_sample[s0:s0 + NSMP, 0], 16)], axis=1)
        xin = np.ascontiguousarray(xin.transpose(0, 2, 1))
        sp = _fm(g('state_pool')[:, s0:s0 + NSMP], 8)
        sp = np.concatenate([sp, np.zeros_like(sp[:, :, :, :1])], axis=3)
        sp = np.ascontiguousarray(sp.transpose(1, 0, 4, 2, 3).reshape(2, 128, -1))
        sc = _fm(g('state_conv_c')[:, s0:s0 + NSMP], 8)
        sc = np.concatenate([sc, np.zeros_like(sc[:, :, :, :1])], axis=3)
        sc = np.ascontiguousarray(sc.transpose(1, 0, 4, 2, 3).reshape(2, 128, -1))
        sd = _fm(g('state_dn_conv')[:, s0:s0 + NSMP], 24)
        sd = np.concatenate([sd, np.zeros_like(sd[:, :, :, :1])], axis=3)
        sd = np.ascontiguousarray(sd.transpose(1, 0, 4, 2, 3).reshape(2, 128, -1))
        ss = g('state_dn_S')[:, s0:s0 + NSMP]
        ss = np.ascontiguousarray(ss.transpose(0, 1, 3, 2, 4).reshape(2, NSMP, 128, -1))
        in_maps.append(dict(xin=xin, spool=sp, sconv=sc, sdnc=sd, sS=ss, ev_w_in=g('ev_w_in'), ev_w_out=g('ev_w_out'),
                            od_w_in=g('od_w_in'), od_w_out=g('od_w_out'), ffn_w_up=g('ffn_w_up'), ffn_w_down=g('ffn_w_down'),
                            prm=prm, lnrep=lnrep, poolw=poolw, wsT=wsT, cst=cst))
    res = run_bass_kernel_spmd(nc, in_maps, core_ids=list(range(8)))
    R = res.results

    def unfm(a):
        a = np.moveaxis(a, 0, -1)
        a = np.moveaxis(a, 0, -2)
        return a.reshape(a.shape[:-2] + (-1,))
    y_prompt = np.zeros((4, 2048, D), f32)
    y_sample = np.zeros((128, 1, D), f32)
    pool_p = np.zeros((2, 4, 15, 1024), f32)
    pool_s = np.zeros((2, 128, 15, 1024), f32)
    v_s = np.zeros((2, 128, 1, 1024), f32)
    cc_p = np.zeros((2, 4, 30, 1024), f32)
    cc_s = np.zeros((2, 128, 30, 1024), f32)
    dc_p = np.zeros((2, 4, 3, 3072), f32)
    dc_s = np.zeros((2, 128, 3, 3072), f32)
    S_p = np.zeros((2, 4, 8, 128, 128), f32)
    S_s = np.zeros((2, 128, 8, 128, 128), f32)
    for c in range(8):
        r = R[c]
        b = c // 2
        s0 = c * NSMP
        yo = unfm(r['yout'])
        if c % 2 == 0:
            y_prompt[b] = yo[:2048]
            for e in range(2):
                pool_p[e, b] = unfm(r['o_pool_p'][e].reshape(128, 8, 15))
                cc_p[e, b] = unfm(r['o_conv_p'][e].reshape(128, 8, 30))
                dc_p[e, b] = unfm(r['o_dnc_p'][e].reshape(128, 24, 3))
                S_p[e, b] = r['o_S_p'][e].reshape(128, 8, 128).transpose(1, 0, 2)
        y_sample[s0:s0 + NSMP, 0] = yo[2048:]
        for e in range(2):
            pool_s[e, s0:s0 + NSMP] = unfm(r['o_pool_s'][e].reshape(128, 8, NSMP, 16)[:, :, :, 1:])
            v_s[e, s0:s0 + NSMP, 0] = r['o_v_s'][e]
            cc_s[e, s0:s0 + NSMP] = unfm(r['o_conv_s'][e].reshape(128, 8, NSMP, 31)[:, :, :, 1:])
            dc_s[e, s0:s0 + NSMP] = unfm(r['o_dnc_s'][e].reshape(128, 24, NSMP, 4)[:, :, :, 1:])
            S_s[e, s0:s0 + NSMP] = r['o_S_s'][e].reshape(NSMP, 128, 8, 128).transpose(0, 2, 1, 3)
    return (y_prompt, y_sample, pool_p, pool_s, v_s, cc_p, cc_s, dc_p, dc_s, S_p, S_s)
```
